# Optimizing a Trainium2 kernel written in Bass

```python
import math
import jax, jax.numpy as jnp
from jax import lax
import numpy as np

D_MODEL = 1024
BATCH = 4
SEQ = 4096
DEPTH = 4

GRID_W = 64
CTX_LEN = 256
EPS = 1e-6

ATT_HEADS = 8
ATT_KV_HEADS = 2
ATT_HEAD_DIM = 64
ATT_WINDOW = 128
ATT_BLOCK = 128
ROPE_THETA = 10000.0
ML_HEADS = 4
ML_HEAD_DIM = 128
ML_CHUNK = 128
HY_WIDTH = 512
HY_ORDER = 2
HY_EMB_BANDS = 16
HY_EMB_DIM = 1 + 2 * HY_EMB_BANDS
HY_FILTER_WIDTH = 64
HY_SHORT = 3
HY_DECAY_TARGET = 1e-2
HY_DECAY_FAST = 0.3
HY_DECAY_SLOW = 1.5
D_FF = 2816
FFN_CONV = 3

ATT_Q = ATT_HEADS * ATT_HEAD_DIM
ATT_KV = ATT_KV_HEADS * ATT_HEAD_DIM
ML_W = ML_HEADS * ML_HEAD_DIM
ML_GATES = 2 * 2 * ML_HEADS
N_BRANCH = 3
BRANCH_W = 512
IN_SIZES = (ATT_Q, ATT_KV, ATT_KV, ML_W, ML_W, ML_W, ML_W, ML_GATES, 3 * HY_WIDTH, N_BRANCH * D_MODEL)
IN_WIDTH = sum(IN_SIZES)
IN_OFFSETS = tuple(int(o) for o in np.cumsum(IN_SIZES)[:-1])

kernel_name = 'hybrid_attn_mlstm_hyena_dit'

F32 = jnp.float32


def rms_norm(x, g):
    xf = x.astype(F32)
    y = xf * lax.rsqrt(jnp.mean(xf * xf, axis=-1, keepdims=True) + EPS)
    return (y * g.astype(F32)).astype(x.dtype)


def dwconv_centered(x, w, b):
    K = w.shape[0]
    pad = K // 2
    L = x.shape[1]
    xp = jnp.pad(x, ((0, 0), (pad, pad), (0, 0)))
    out = b + xp[:, 0:L] * w[0]
    for j in range(1, K):
        out = out + xp[:, j:j + L] * w[j]
    return out


def split_heads(t, n):
    return t.reshape(*t.shape[:-1], n, -1)


def axial_rope_angles(L):
    rows = L // GRID_W
    row = jnp.repeat(jnp.arange(rows, dtype=F32), GRID_W)
    col = jnp.tile(jnp.arange(GRID_W, dtype=F32), rows)
    nf = ATT_HEAD_DIM // 4
    inv = ROPE_THETA ** (-jnp.arange(nf, dtype=F32) / nf)
    ang = jnp.concatenate([row[:, None] * inv, col[:, None] * inv], axis=-1)
    return jnp.cos(ang), jnp.sin(ang)


def apply_rope(t, cos, sin):
    half = t.shape[-1] // 2
    t1 = t[..., :half].astype(F32)
    t2 = t[..., half:].astype(F32)
    c = cos[:, None, :]
    s = sin[:, None, :]
    return jnp.concatenate([t1 * c - t2 * s, t2 * c + t1 * s], axis=-1).astype(t.dtype)


def window_attention(q, k, v, kc, vc, sink):
    B, L, Hq, hd = q.shape
    Hkv = k.shape[2]
    G = Hq // Hkv
    T = ATT_BLOCK
    nb = L // T
    Lc = kc.shape[1]
    scale = hd ** -0.5
    qb = q.reshape(B, nb, T, Hkv, G, hd)

    def band(t):
        tp = jnp.pad(t, ((0, 0), (T, T), (0, 0), (0, 0))).reshape(B, nb + 2, T, Hkv, hd)
        return jnp.concatenate([tp[:, :-2], tp[:, 1:-1], tp[:, 2:]], axis=2)

    kb, vb = band(k), band(v)
    blk = jnp.arange(nb)[:, None, None]
    qpos = blk * T + jnp.arange(T)[None, :, None]
    kpos = (blk - 1) * T + jnp.arange(3 * T)[None, None, :]
    valid = (jnp.abs(kpos - qpos) <= ATT_WINDOW) & (kpos >= 0) & (kpos < L)
    s_loc = jnp.einsum('bntkgd,bnskd->bnkgts', qb, kb).astype(F32) * scale
    s_loc = jnp.where(valid[None, :, None, None], s_loc, -jnp.inf)
    s_ctx = jnp.einsum('bntkgd,bckd->bnkgtc', qb, kc).astype(F32) * scale
    s_sink = jnp.broadcast_to(sink.astype(F32).reshape(Hkv, G, 1, 1), s_ctx.shape[:-1] + (1,))
    p = jax.nn.softmax(jnp.concatenate([s_sink, s_ctx, s_loc], axis=-1), axis=-1)
    p_ctx = p[..., 1:1 + Lc].astype(v.dtype)
    p_loc = p[..., 1 + Lc:].astype(v.dtype)
    o = jnp.einsum('bnkgts,bnskd->bntkgd', p_loc, vb) + jnp.einsum('bnkgtc,bckd->bntkgd', p_ctx, vc)
    return o.reshape(B, L, Hq * hd).astype(v.dtype)


def context_attention(q, k, v, sink):
    B, Lc, Hq, hd = q.shape
    Hkv = k.shape[2]
    G = Hq // Hkv
    qg = q.reshape(B, Lc, Hkv, G, hd)
    s = jnp.einsum('btkgd,bskd->bkgts', qg, k).astype(F32) * hd ** -0.5
    s_sink = jnp.broadcast_to(sink.astype(F32).reshape(Hkv, G, 1, 1), s.shape[:-1] + (1,))
    p = jax.nn.softmax(jnp.concatenate([s_sink, s], axis=-1), axis=-1)[..., 1:].astype(v.dtype)
    o = jnp.einsum('bkgts,bskd->btkgd', p, v)
    return o.reshape(B, Lc, Hq * hd).astype(v.dtype)


def mlstm_scan(q, k, v, i_pre, logf, state):
    B, H, L, d = q.shape
    T = ML_CHUNK
    nc = L // T

    def chunks(a):
        return jnp.moveaxis(a.reshape(B, H, nc, T, *a.shape[3:]), 2, 0)

    tril = jnp.tril(jnp.ones((T, T), dtype=bool))

    def step(carry, inp):
        C, n, m = carry
        qc, kc, vc, ic, fc = inp
        b = jnp.cumsum(fc, axis=-1)
        a = b + m[..., None]
        dmat = jnp.where(tril, b[..., :, None] - b[..., None, :] + ic[..., None, :], -jnp.inf)
        mt = jnp.maximum(a, jnp.max(dmat, axis=-1))
        w_inter = jnp.exp(a - mt)
        s = jnp.einsum('bhtd,bhsd->bhts', qc, kc) * jnp.exp(dmat - mt[..., None])
        num = w_inter[..., None] * jnp.einsum('bhvk,bhtk->bhtv', C, qc) + jnp.einsum('bhts,bhsv->bhtv', s, vc)
        den = w_inter * jnp.einsum('bhk,bhtk->bht', n, qc) + jnp.sum(s, axis=-1)
        h = num / jnp.maximum(jnp.abs(den), jnp.exp(-mt))[..., None]
        bl = b[..., -1]
        src = bl[..., None] - b + ic
        m_new = jnp.maximum(bl + m, jnp.max(src, axis=-1))
        g = jnp.exp(src - m_new[..., None])
        decay = jnp.exp(bl + m - m_new)
        C_new = decay[..., None, None] * C + jnp.einsum('bhs,bhsv,bhsk->bhvk', g, vc, kc)
        n_new = decay[..., None] * n + jnp.einsum('bhs,bhsk->bhk', g, kc)
        return (C_new, n_new, m_new), h

    state, hs = lax.scan(step, state, (chunks(q), chunks(k), chunks(v), chunks(i_pre), chunks(logf)))
    return jnp.moveaxis(hs, 0, 2).reshape(B, H, L, d), state


def mlstm_inputs(p, gate_b):
    B, L, _ = p[3].shape

    def to_bhld(t):
        return split_heads(t, ML_HEADS).astype(F32).transpose(0, 2, 1, 3)

    q = to_bhld(p[3])
    k = to_bhld(p[4]) * ML_HEAD_DIM ** -0.5
    v = to_bhld(p[5])
    g = p[7].astype(F32).reshape(B, L, 2, 2, ML_HEADS) + gate_b.astype(F32)
    g = jnp.moveaxis(g, 1, -1)
    return q, k, v, g[:, :, 0], jax.nn.log_sigmoid(g[:, :, 1])


def mlstm_bidir(ctx_in, lat_in, need_ctx):
    qc, kc, vc, ic, fc = ctx_in
    ql, kl, vl, il, fl = lat_in
    B, H, _, d = ql.shape
    zero = (jnp.zeros((B, H, d, d), F32), jnp.zeros((B, H, d), F32), jnp.zeros((B, H), F32))

    def flip(t):
        return jnp.flip(t, axis=2)

    hcf, sf = mlstm_scan(qc, kc, vc, ic[:, 0], fc[:, 0], zero)
    hlf, _ = mlstm_scan(ql, kl, vl, il[:, 0], fl[:, 0], sf)
    hcb, sb = mlstm_scan(flip(qc), flip(kc), flip(vc), flip(ic[:, 1]), flip(fc[:, 1]), zero)
    hlb, _ = mlstm_scan(flip(ql), flip(kl), flip(vl), flip(il[:, 1]), flip(fl[:, 1]), sb)
    h_lat = hlf + flip(hlb)
    h_ctx = hcf + flip(hcb) if need_ctx else None
    return h_lat, h_ctx


def mlstm_output(h, o_pre, g):
    B, H, L, d = h.shape
    h = h.transpose(0, 2, 1, 3)
    y = h * lax.rsqrt(jnp.mean(h * h, axis=-1, keepdims=True) + EPS) * g.astype(F32).reshape(H, d)
    y = y * jax.nn.sigmoid(split_heads(o_pre, ML_HEADS).astype(F32))
    return y.reshape(B, L, H * d).astype(o_pre.dtype)


def hyena_filters(L, w1, b1, w2, b2, w3, freq):
    t = jnp.arange(L, dtype=F32)
    t_norm = t / max(L - 1, 1)
    w = 2.0 * math.pi * t / L
    bands = jnp.linspace(1e-4, HY_EMB_BANDS - 1, HY_EMB_BANDS, dtype=F32)
    z = jnp.concatenate([t_norm[:, None], jnp.cos(w[:, None] * bands), -jnp.sin(w[:, None] * bands)], axis=-1)
    z = z.astype(w1.dtype)
    h = jnp.sin(freq[0] * (z @ w1 + b1))
    h = jnp.sin(freq[1] * (h @ w2 + b2))
    h = (h @ w3).astype(F32).reshape(L, HY_ORDER, 2, HY_WIDTH)
    deltas = jnp.abs(jnp.linspace(math.log(HY_DECAY_TARGET) / HY_DECAY_SLOW,
                                  math.log(HY_DECAY_TARGET) / HY_DECAY_FAST, HY_WIDTH, dtype=F32))
    h = h * jnp.exp(-t_norm[:, None] * deltas)[:, None, None, :]
    return h / jnp.sum(jnp.abs(h), axis=0, keepdims=True)


def bidir_long_conv(u, hf, hb, skip):
    L = u.shape[1]
    n = 2 * L
    U = jnp.fft.rfft(u.astype(F32), n=n, axis=1)
    Hf = jnp.fft.rfft(hf, n=n, axis=0) + jnp.conj(jnp.fft.rfft(hb, n=n, axis=0))
    y = jnp.fft.irfft(U * Hf[None], n=n, axis=1)[:, :L]
    return (y + u.astype(F32) * skip.astype(F32)).astype(u.dtype)


def hyena_operator(u, filt, skip):
    x1, x2, v = jnp.split(u, 3, axis=-1)
    z = v
    for o, gate in enumerate((x1, x2)):
        z = gate * bidir_long_conv(z, filt[:, o, 0], filt[:, o, 1], skip[o])
    return z


def merge_branches(branches, gate_pre, w_branch, w_out):
    gates = jax.nn.sigmoid(gate_pre).reshape(*gate_pre.shape[:-1], N_BRANCH, -1)
    y = gates[..., 0, :] * (branches[0] @ w_branch[0])
    for r in range(1, N_BRANCH):
        y = y + gates[..., r, :] * (branches[r] @ w_branch[r])
    return y @ w_out


def token_mixers(hl, hc, cos, sin, w_in, sink, gate_b, ml_g, hy_sw, hy_sb, hy_fp, hy_skip, w_branch, w_out, need_ctx):
    L = hl.shape[1]
    Lc = hc.shape[1]
    pl = jnp.split(hl @ w_in, IN_OFFSETS, axis=-1)
    pc = jnp.split(hc @ w_in, IN_OFFSETS, axis=-1)
    ql = apply_rope(split_heads(pl[0], ATT_HEADS), cos, sin)
    kl = apply_rope(split_heads(pl[1], ATT_KV_HEADS), cos, sin)
    vl = split_heads(pl[2], ATT_KV_HEADS)
    kc = split_heads(pc[1], ATT_KV_HEADS)
    vc = split_heads(pc[2], ATT_KV_HEADS)
    att_l = window_attention(ql, kl, vl, kc, vc, sink)
    h_lat, h_ctx = mlstm_bidir(mlstm_inputs(pc, gate_b), mlstm_inputs(pl, gate_b), need_ctx)
    mls_l = mlstm_output(h_lat, pl[6], ml_g)
    hy_l = hyena_operator(dwconv_centered(pl[8], hy_sw, hy_sb), hyena_filters(L, *hy_fp), hy_skip)
    y_l = merge_branches((att_l, mls_l, hy_l), pl[9], w_branch, w_out)
    if not need_ctx:
        return y_l, None
    att_c = context_attention(split_heads(pc[0], ATT_HEADS), kc, vc, sink)
    mls_c = mlstm_output(h_ctx, pc[6], ml_g)
    hy_c = hyena_operator(dwconv_centered(pc[8], hy_sw, hy_sb), hyena_filters(Lc, *hy_fp), hy_skip)
    y_c = merge_branches((att_c, mls_c, hy_c), pc[9], w_branch, w_out)
    return y_l, y_c


def conv_ffn(h, w_up, cw, cb, w_down):
    u = dwconv_centered(h @ w_up, cw, cb)
    gate, val = jnp.split(u, 2, axis=-1)
    return (jax.nn.silu(gate) * val) @ w_down


def setup_inputs(seed: int = 0) -> dict:
    key = jax.random.key(seed)
    kit = iter(jax.random.split(key, 40))
    D = D_MODEL

    def nrm(shape, scale):
        return jax.random.normal(next(kit), shape, F32) * scale

    i_b = nrm((DEPTH, 2, ML_HEADS), 0.1)
    f_b = jnp.linspace(3.0, 6.0, ML_HEADS, dtype=F32) + nrm((DEPTH, 2, ML_HEADS), 0.1)
    return {
        'x': nrm((BATCH, SEQ, D), 1.0),
        'c': nrm((BATCH, D), 1.0),
        'ctx': nrm((BATCH, CTX_LEN, D), 1.0),
        'c_ctx': nrm((D,), 1.0),
        'ada_w': nrm((DEPTH, D, 6 * D), 0.5 * D ** -0.5),
        'ada_b': nrm((DEPTH, 6 * D), 0.02),
        'norm1_g': 1.0 + nrm((DEPTH, D), 0.05),
        'norm2_g': 1.0 + nrm((DEPTH, D), 0.05),
        'w_in': nrm((DEPTH, D, IN_WIDTH), D ** -0.5),
        'att_sink': nrm((DEPTH, ATT_HEADS), 0.5),
        'ml_gate_b': jnp.stack([i_b, f_b], axis=2),
        'ml_norm_g': 1.0 + nrm((DEPTH, ML_W), 0.05),
        'hy_short_w': nrm((DEPTH, HY_SHORT, 3 * HY_WIDTH), HY_SHORT ** -0.5),
        'hy_short_b': nrm((DEPTH, 3 * HY_WIDTH), 0.02),
        'hy_w1': nrm((DEPTH, HY_EMB_DIM, HY_FILTER_WIDTH), HY_EMB_DIM ** -0.5),
        'hy_b1': nrm((DEPTH, HY_FILTER_WIDTH), 0.02),
        'hy_w2': nrm((DEPTH, HY_FILTER_WIDTH, HY_FILTER_WIDTH), HY_FILTER_WIDTH ** -0.5),
        'hy_b2': nrm((DEPTH, HY_FILTER_WIDTH), 0.02),
        'hy_w3': nrm((DEPTH, HY_FILTER_WIDTH, HY_ORDER * 2 * HY_WIDTH), HY_FILTER_WIDTH ** -0.5),
        'hy_freq': 1.0 + nrm((DEPTH, 2, HY_FILTER_WIDTH), 0.1),
        'hy_skip': nrm((DEPTH, HY_ORDER, HY_WIDTH), 0.5),
        'w_branch': nrm((DEPTH, N_BRANCH, BRANCH_W, D), BRANCH_W ** -0.5),
        'w_out': nrm((DEPTH, D, D), D ** -0.5),
        'w_up': nrm((DEPTH, D, 2 * D_FF), D ** -0.5),
        'ffn_conv_w': nrm((DEPTH, FFN_CONV, 2 * D_FF), FFN_CONV ** -0.5),
        'ffn_conv_b': nrm((DEPTH, 2 * D_FF), 0.02),
        'w_down': nrm((DEPTH, D_FF, D), D_FF ** -0.5),
        'final_g': 1.0 + nrm((D,), 0.05),
    }


def reference(x, c, ctx, c_ctx, ada_w, ada_b, norm1_g, norm2_g, w_in, att_sink, ml_gate_b, ml_norm_g,
              hy_short_w, hy_short_b, hy_w1, hy_b1, hy_w2, hy_b2, hy_w3, hy_freq, hy_skip,
              w_branch, w_out, w_up, ffn_conv_w, ffn_conv_b, w_down, final_g):
    L = x.shape[1]
    cos, sin = axial_rope_angles(L)
    sc = jax.nn.silu(c)
    scc = jax.nn.silu(c_ctx)
    xl, xc = x, ctx
    for l in range(DEPTH):
        need_ctx = l < DEPTH - 1
        sh1, sc1, g1, sh2, sc2, g2 = [m[:, None, :] for m in jnp.split(sc @ ada_w[l] + ada_b[l], 6, axis=-1)]
        csh1, csc1, cg1, csh2, csc2, cg2 = jnp.split(scc @ ada_w[l] + ada_b[l], 6, axis=-1)
        hl = rms_norm(xl, norm1_g[l]) * (1.0 + sc1) + sh1
        hc = rms_norm(xc, norm1_g[l]) * (1.0 + csc1) + csh1
        hy_fp = (hy_w1[l], hy_b1[l], hy_w2[l], hy_b2[l], hy_w3[l], hy_freq[l])
        yl, yc = token_mixers(hl, hc, cos, sin, w_in[l], att_sink[l], ml_gate_b[l], ml_norm_g[l],
                              hy_short_w[l], hy_short_b[l], hy_fp, hy_skip[l], w_branch[l], w_out[l], need_ctx)
        xl = xl + g1 * yl
        hl = rms_norm(xl, norm2_g[l]) * (1.0 + sc2) + sh2
        xl = xl + g2 * conv_ffn(hl, w_up[l], ffn_conv_w[l], ffn_conv_b[l], w_down[l])
        if need_ctx:
            xc = xc + cg1 * yc
            hc = rms_norm(xc, norm2_g[l]) * (1.0 + csc2) + csh2
            xc = xc + cg2 * conv_ffn(hc, w_up[l], ffn_conv_w[l], ffn_conv_b[l], w_down[l])
    return rms_norm(xl, final_g)
```

```python
import math
from contextlib import ExitStack
import numpy as np
import ml_dtypes
import concourse.bass as bass
import concourse.mybir as mybir
from concourse.bass_utils import run_bass_kernel_spmd

F32 = mybir.dt.float32
BF16 = mybir.dt.bfloat16
ALU = mybir.AluOpType
ACT = mybir.ActivationFunctionType
AX = mybir.AxisListType

SAME_ENGINE_SYNC = True

D = 1024
KC = 8
L = 4096
LC = 256
NT = L + LC
NTILE = NT // 128
DEPTH = 4
EPS = 1e-6
IN_W = 7440
DFF = 2816
O_Q, O_K, O_V, O_MQ, O_MK, O_MV, O_MO, O_MG, O_HY, O_BG = 0, 512, 640, 768, 1280, 1792, 2304, 2816, 2832, 4368
BLOCKS = [(0, 256)] + [(256 + 512 * i, 512) for i in range(8)]


class Buf:
    __slots__ = ("name", "last_w", "readers")

    def __init__(self, name=""):
        self.name = name
        self.last_w = None
        self.readers = {}


class Sched:
    COMPUTE = ("tensor", "vector", "scalar", "gpsimd")
    QUEUES = {"sync": 24, "gpsimd": 12, "scalar": 8}

    def __init__(self, nc, es):
        self.nc = nc
        self.streams = {e: [] for e in ("tensor", "vector", "scalar", "gpsimd", "sync")}
        self.sems = {}
        self.cnt = {}
        for e in self.COMPUTE:
            self.sems[("E", e)] = es.enter_context(nc.semaphore("s_" + e))
            self.cnt[("E", e)] = 0
        self.dq = {}
        for q, n in self.QUEUES.items():
            keys = []
            for i in range(n):
                k = ("D", q, i)
                self.sems[k] = es.enter_context(nc.semaphore("d_%s%d" % (q, i)))
                self.cnt[k] = 0
                keys.append(k)
            self.dq[q] = [keys, 0]
        self.waited = {}
        self.nops = 0

    def _deps(self, eng, reads, writes, is_dma=False):
        deps = {}

        def add(k, v, e):
            if k in deps:
                if deps[k][0] < v:
                    deps[k] = (v, e)
            else:
                deps[k] = (v, e)
        for b in reads:
            if b.last_w is not None:
                add(*b.last_w)
        for b in writes:
            if b.last_w is not None:
                add(*b.last_w)
            for k, (v, e) in b.readers.items():
                add(k, v, e)
        waits = []
        for k, (v, e) in deps.items():
            if k[0] == "E" and e == eng:
                if not is_dma and (eng == "tensor" or not SAME_ENGINE_SYNC):
                    continue
            wk = (eng, k)
            if self.waited.get(wk, -1) >= v:
                continue
            self.waited[wk] = v
            waits.append((k, v))
        return waits

    def _commit(self, tok, reads, writes):
        k, v, e = tok
        for b in reads:
            b.readers[k] = (v, e)
        for b in writes:
            b.last_w = tok
            b.readers = {}

    def op(self, eng, name, *args, R=(), W=(), **kw):
        reads, writes = R, W

        def fn(e, name=name, args=args, kw=kw):
            return getattr(e, name)(*args, **kw)
        waits = self._deps(eng, reads, writes)
        k = ("E", eng)
        self.cnt[k] += 1
        tok = (k, self.cnt[k], eng)
        self.streams[eng].append((waits, fn, k, 1))
        self._commit(tok, reads, writes)
        self.nops += 1
        return tok

    def dma(self, q, out, in_, reads=(), writes=(), **kw):
        reads = [b for b in reads if b is not None]
        writes = [b for b in writes if b is not None]
        keys, idx = self.dq[q]
        k = keys[idx % len(keys)]
        self.dq[q][1] = idx + 1
        waits = self._deps(q, reads, writes, is_dma=True)
        prev = self.cnt[k]
        if prev > 0:
            wk = (q, k)
            if self.waited.get(wk, -1) < prev:
                self.waited[wk] = prev
                waits.append((k, prev))
        self.cnt[k] += 16
        tok = (k, self.cnt[k], q)

        def fn(e, out=out, in_=in_, kw=kw):
            return e.dma_start(out=out, in_=in_, **kw)
        self.streams[q].append((waits, fn, k, 16))
        self._commit(tok, reads, writes)
        self.nops += 1
        return tok

    def barrier(self):
        for eng in self.streams:
            waits = []
            for k, v in self.cnt.items():
                if v == 0:
                    continue
                if k[0] == "E" and k[1] == eng:
                    continue
                wk = (eng, k)
                if self.waited.get(wk, -1) >= v:
                    continue
                self.waited[wk] = v
                waits.append((k, v))
            if waits:
                self.streams[eng].append((waits, None, None, 0))

    def emit(self):
        nc = self.nc
        sems = self.sems

        def replay(e, name):
            for waits, fn, k, inc in self.streams[name]:
                for (wk, v) in waits:
                    e.wait_ge(sems[wk], v)
                if fn is not None:
                    ins = fn(e)
                    ins.then_inc(sems[k], inc)
        with nc.Block() as block:
            @block.sync
            def _(e):
                replay(e, "sync")

            @block.tensor
            def _(e):
                replay(e, "tensor")

            @block.vector
            def _(e):
                replay(e, "vector")

            @block.scalar
            def _(e):
                replay(e, "scalar")

            @block.gpsimd
            def _(e):
                replay(e, "gpsimd")


class Ctx:
    def __init__(self, nc, S):
        self.nc = nc
        self.S = S
        self.psi = 0
        self.dq_i = 0
        self.reserved = set()

    def ps(self):
        while True:
            i = self.psi % 8
            self.psi += 1
            if i not in self.reserved:
                return self.psum[i], self.psb[i]

    def ldq(self):
        self.dq_i += 1
        return "sync" if self.dq_i % 3 else "gpsimd"


class Stage:
    uid = 0

    def __init__(self, C):
        self.C = C
        self.es = ExitStack()
        self.n = 0

    def __enter__(self):
        self.es.__enter__()
        return self

    def __exit__(self, *a):
        self.C.S.barrier()
        return self.es.__exit__(*a)

    def tile(self, shape, dtype, name=None):
        self.n += 1
        Stage.uid += 1
        nm = "%s_%d" % (name or "t", Stage.uid)
        t = self.es.enter_context(self.C.nc.sbuf_tensor(nm, list(shape), dtype))
        return t, Buf(nm)


def stage_mod(C, I, mod, mod_b):
    S = C.S
    with Stage(C) as st:
        cv, cvb = st.tile([128, KC, 2], F32, "cv")
        S.dma("sync", cv[:], I["cvec"][:, :, :], writes=[cvb])
        scv, scvb = st.tile([128, KC, 2], F32, "scv")
        S.op("scalar", "activation", out=scv[:], in_=cv[:], func=ACT.Silu, R=[cvb], W=[scvb])
        wbufs = [st.tile([128, KC, 1024], F32, "adaw") for _ in range(2)]
        adab, adabb = st.tile([128, DEPTH, 48], F32, "adab")
        S.dma("sync", adab[:], I["ada_bT"][:, :, :], writes=[adabb])
        it = 0
        for l in range(DEPTH):
            wv = I["ada_w"][l].rearrange("(k p) n -> p k n", p=128)
            pst, psb = C.ps()
            for slab in range(6):
                wt, wtb = wbufs[it % 2]
                it += 1
                for k in range(KC):
                    S.dma("sync" if k % 2 else "gpsimd", wt[:, k, :], wv[:, k, slab * 1024:(slab + 1) * 1024], writes=[wtb])
                for jj in range(8):
                    j = slab * 8 + jj
                    for k in range(KC):
                        S.op("tensor", "matmul", pst[:, 2 * j:2 * j + 2], lhsT=wt[:, k, jj * 128:(jj + 1) * 128], rhs=scv[:, k, :],
                             start=(k == 0), stop=(k == KC - 1), R=[wtb, scvb], W=[psb])
            for w in range(2):
                S.op("vector", "tensor_tensor", out=mod[l][:, :, w],
                     in0=pst[:, 0:96].rearrange("p (j w) -> p j w", w=2)[:, :, w], in1=adab[:, l, :], op=ALU.add,
                     R=[psb, adabb], W=[mod_b[l]])


def load_weight_bf16(C, wsrc, ncols, pool, kc=KC):
    S = C.S
    i = pool["i"]
    pool["i"] += 1
    wf, wfb = pool["f"][i % len(pool["f"])]
    wb, wbb = pool["b"][i % len(pool["b"])]
    half = kc // 2
    S.dma("sync", wf[:, 0:half, 0:ncols], wsrc[:, 0:half, :], writes=[wfb])
    S.dma("gpsimd", wf[:, half:kc, 0:ncols], wsrc[:, half:kc, :], writes=[wfb])
    S.op("gpsimd", "tensor_copy", out=wb[:, 0:half, 0:ncols], in_=wf[:, 0:half, 0:ncols], R=[wfb], W=[wbb])
    S.op("vector", "tensor_copy", out=wb[:, half:kc, 0:ncols], in_=wf[:, half:kc, 0:ncols], R=[wfb], W=[wbb])
    return wb, wbb


def make_wpool(st, kc=KC, ncols=512, nbuf=2):
    return {"i": 0, "f": [st.tile([128, kc, ncols], F32, "wf") for _ in range(nbuf)],
            "b": [st.tile([128, kc, ncols], BF16, "wb") for _ in range(nbuf)]}


def rms_rstd(C, st, xt, xtb, n, rs, rsb, sqs):
    S = C.S
    pst, psb = C.ps()
    for k in range(KC):
        sq, sqb = sqs[k % 2]
        S.op("scalar", "activation", out=sq[:, 0:n], in_=xt[:, k, 0:n], func=ACT.Square, R=[xtb], W=[sqb])
        S.op("tensor", "matmul", pst[:, 0:n], lhsT=C.ones_bf[:], rhs=sq[:, 0:n], start=(k == 0), stop=(k == KC - 1),
             R=[sqb, C.constb], W=[psb])
    S.op("scalar", "activation", out=rs[:, 0:n], in_=pst[:, 0:n], func=ACT.Ln, bias=EPS, scale=1.0 / D, R=[psb], W=[rsb])
    S.op("scalar", "activation", out=rs[:, 0:n], in_=rs[:, 0:n], func=ACT.Exp, scale=-0.5, R=[rsb], W=[rsb])


def norm_mod(C, st, xsrc, hT, hTb, A, B, Ab, blocks):
    S = C.S
    xb = [st.tile([128, KC, 512], F32, "xblk") for _ in range(2)]
    sqs = [st.tile([128, 512], BF16, "sq") for _ in range(2)]
    rs, rsb = st.tile([128, 512], F32, "rstd")
    tm = [st.tile([128, 512], F32, "tmpn") for _ in range(2)]
    xv = xsrc.rearrange("(k p) n -> p k n", p=128)
    for bi, (t0, n) in blocks:
        w = 1 if bi == 0 else 0
        xt, xtb = xb[bi % 2]
        S.dma("sync", xt[:, 0:4, 0:n], xv[:, 0:4, t0:t0 + n], reads=[C.scrb], writes=[xtb])
        S.dma("gpsimd", xt[:, 4:8, 0:n], xv[:, 4:8, t0:t0 + n], reads=[C.scrb], writes=[xtb])
        rms_rstd(C, st, xt, xtb, n, rs, rsb, sqs)
        for k in range(KC):
            t, tb = tm[k % 2]
            S.op("vector", "scalar_tensor_tensor", out=t[:, 0:n], in0=xt[:, k, 0:n], scalar=A[:, k, w:w + 1], in1=rs[:, 0:n],
                 op0=ALU.mult, op1=ALU.mult, R=[xtb, rsb, Ab], W=[tb])
            S.op("scalar", "activation", out=hT[:, k, t0:t0 + n], in_=t[:, 0:n], func=ACT.Identity, bias=B[:, k, w:w + 1], scale=1.0,
                 R=[tb, Ab], W=[hTb])


def proj_fm(C, hT, hTb, wb, wbb, col0, M, blocks, epilogue, kc=KC):
    S = C.S
    for bi, (t0, n) in blocks:
        pst, psb = C.ps()
        for k in range(kc):
            S.op("tensor", "matmul", pst[0:M, 0:n], lhsT=wb[:, k, col0:col0 + M], rhs=hT[:, k, t0:t0 + n],
                 start=(k == 0), stop=(k == kc - 1), R=[wbb, hTb], W=[psb])
        epilogue(bi, t0, n, pst, psb)


def proj_tm(C, hT, hTb, wb, wbb, col0, ncols, epilogue, kc=KC, tiles=None):
    S = C.S
    for ti in (tiles if tiles is not None else range(NTILE)):
        pst, psb = C.ps()
        for k in range(kc):
            S.op("tensor", "matmul", pst[:, 0:ncols], lhsT=hT[:, k, ti * 128:(ti + 1) * 128], rhs=wb[:, k, col0:col0 + ncols],
                 start=(k == 0), stop=(k == kc - 1), R=[wbb, hTb], W=[psb])
        epilogue(ti, pst, psb)


class Stager:
    def __init__(self, st, shape, dtype, n=3, name="stg"):
        self.bufs = [st.tile(shape, dtype, name) for _ in range(n)]
        self.i = 0

    def get(self):
        t = self.bufs[self.i % len(self.bufs)]
        self.i += 1
        return t


def stage_inproj(C, I, l, xsrc, A, B, Ab, SC):
    S = C.S
    ALLB = list(enumerate(BLOCKS))
    with Stage(C) as st:
        hT, hTb = st.tile([128, KC, NT], BF16, "hT")
        norm_mod(C, st, xsrc, hT, hTb, A, B, Ab, ALLB)
        wpool = make_wpool(st)
        wv = I["w_in"][l].rearrange("(k p) n -> p k n", p=128)
        stg_b = Stager(st, [128, 512], BF16, 4, "stgb")
        stg_f = Stager(st, [128, 512], F32, 3, "stgf")
        ev = [0]

        def evac_copy(dst_dram, stager, scale=None, func=None):
            def ep(bi, t0, n, pst, psb):
                sb, sbb = stager.get()
                ev[0] += 1
                if func is not None:
                    S.op("scalar", "activation", out=sb[:, 0:n], in_=pst[:, 0:n], func=func, R=[psb], W=[sbb])
                elif scale is not None:
                    S.op("scalar", "mul", out=sb[:, 0:n], in_=pst[:, 0:n], mul=scale, R=[psb], W=[sbb])
                elif ev[0] % 2:
                    S.op("scalar", "copy", out=sb[:, 0:n], in_=pst[:, 0:n], R=[psb], W=[sbb])
                else:
                    S.op("vector", "tensor_copy", out=sb[:, 0:n], in_=pst[:, 0:n], R=[psb], W=[sbb])
                S.dma(C.ldq(), dst_dram[:, t0:t0 + n], sb[:, 0:n], reads=[sbb], writes=[C.scrb])
            return ep

        rope_c = [st.tile([128, 512], F32, "ropec") for _ in range(2)]
        rope_s = [st.tile([128, 512], F32, "ropes") for _ in range(2)]
        rtmp = [st.tile([128, 512], F32, "rtmp") for _ in range(2)]
        rtmp2 = [st.tile([128, 512], F32, "rtmp2") for _ in range(2)]
        for (c0, nchunk, dst) in ((O_Q, 4, SC["qT"]), (O_K, 1, SC["kT"])):
            W = nchunk * 128
            wb, wbb = load_weight_bf16(C, wv[:, :, c0:c0 + W], W, wpool)
            i = wpool["i"]
            wpool["i"] += 1
            wf2, wf2b = wpool["f"][i % 2]
            wb2, wb2b = wpool["b"][i % 2]
            src = wv[:, :, c0:c0 + W].rearrange("p k (h two i) -> p k h two i", two=2, i=32)
            dstv = wf2[:, :, 0:W].rearrange("p k (h two i) -> p k h two i", two=2, i=32)
            for k in range(KC):
                S.dma("sync", dstv[:, k, :, 1, :], src[:, k, :, 0, :], writes=[wf2b])
                S.dma("gpsimd", dstv[:, k, :, 0, :], src[:, k, :, 1, :], writes=[wf2b])
            S.op("gpsimd", "tensor_copy", out=wb2[:, :, 0:W], in_=wf2[:, :, 0:W], R=[wf2b], W=[wb2b])
            for bi, (t0, n) in ALLB:
                rc = rsn = rcb = rsnb = None
                if bi > 0:
                    rc, rcb = rope_c[bi % 2]
                    rsn, rsnb = rope_s[bi % 2]
                    S.dma("sync", rc[:, 0:n], I["ropecos"][:, t0 - LC:t0 - LC + n], writes=[rcb])
                    S.dma("sync", rsn[:, 0:n], I["ropesin"][:, t0 - LC:t0 - LC + n], writes=[rsnb])
                for ch in range(nchunk):
                    pst, psb = C.ps()
                    for k in range(KC):
                        S.op("tensor", "matmul", pst[:, 0:n], lhsT=wb[:, k, ch * 128:(ch + 1) * 128], rhs=hT[:, k, t0:t0 + n],
                             start=(k == 0), stop=(k == KC - 1), R=[wbb, hTb], W=[psb])
                    sb, sbb = stg_b.get()
                    if bi == 0:
                        S.op("scalar", "copy", out=sb[:, 0:n], in_=pst[:, 0:n], R=[psb], W=[sbb])
                    else:
                        pst2, psb2 = C.ps()
                        for k in range(KC):
                            S.op("tensor", "matmul", pst2[:, 0:n], lhsT=wb2[:, k, ch * 128:(ch + 1) * 128], rhs=hT[:, k, t0:t0 + n],
                                 start=(k == 0), stop=(k == KC - 1), R=[wb2b, hTb], W=[psb2])
                        t1, t1b = rtmp[ch % 2]
                        t2, t2b = rtmp2[ch % 2]
                        S.op("vector", "tensor_tensor", out=t1[:, 0:n], in0=pst[:, 0:n], in1=rc[:, 0:n], op=ALU.mult, R=[psb, rcb], W=[t1b])
                        S.op("vector", "tensor_tensor", out=t2[:, 0:n], in0=pst2[:, 0:n], in1=rsn[:, 0:n], op=ALU.mult, R=[psb2, rsnb], W=[t2b])
                        S.op("gpsimd", "tensor_tensor", out=sb[:, 0:n], in0=t1[:, 0:n], in1=t2[:, 0:n], op=ALU.add, R=[t1b, t2b], W=[sbb])
                    S.dma(C.ldq(), dst[ch * 128:(ch + 1) * 128, t0:t0 + n], sb[:, 0:n], reads=[sbb], writes=[C.scrb])

        def fm_group(c0, nchunk, dst, stager, scale=None, func=None):
            done = 0
            while done < nchunk:
                g = min(4, nchunk - done)
                wb, wbb = load_weight_bf16(C, wv[:, :, c0 + done * 128:c0 + (done + g) * 128], g * 128, wpool)
                for ch in range(g):
                    row0 = (done + ch) * 128
                    proj_fm(C, hT, hTb, wb, wbb, ch * 128, 128, ALLB, evac_copy(dst[row0:row0 + 128, :], stager, scale=scale, func=func))
                done += g

        fm_group(O_MQ, 4, SC["mlqT"], stg_b)
        fm_group(O_MK, 4, SC["mlkT"], stg_b, scale=128.0 ** -0.5)
        fm_group(O_HY, 12, SC["hyu"], stg_f)
        fm_group(O_BG, 24, SC["bgT"], stg_b, func=ACT.Sigmoid)

        def tm_group(c0, ncols, dst, stager, scale=None):
            wb, wbb = load_weight_bf16(C, wv[:, :, c0:c0 + ncols], ncols, wpool)

            def ep(ti, pst, psb):
                sb, sbb = stager.get()
                if scale is not None:
                    S.op("scalar", "mul", out=sb[:, 0:ncols], in_=pst[:, 0:ncols], mul=scale, R=[psb], W=[sbb])
                elif ti % 2:
                    S.op("scalar", "copy", out=sb[:, 0:ncols], in_=pst[:, 0:ncols], R=[psb], W=[sbb])
                else:
                    S.op("vector", "tensor_copy", out=sb[:, 0:ncols], in_=pst[:, 0:ncols], R=[psb], W=[sbb])
                S.dma(C.ldq(), dst[ti * 128:(ti + 1) * 128, 0:ncols], sb[:, 0:ncols], reads=[sbb], writes=[C.scrb])
            proj_tm(C, hT, hTb, wb, wbb, 0, ncols, ep)

        tm_group(O_V, 128, SC["vtok"], stg_b)
        tm_group(O_MK, 512, SC["mlktok"], stg_b, scale=128.0 ** -0.5)
        tm_group(O_MV, 512, SC["mlvtok"], stg_b)
        tm_group(O_MO, 512, SC["mlotok"], stg_f)
        tm_group(O_MG, 16, SC["mlgtok"], stg_f)


def stage_attn(C, I, l, SC, need_ctx):
    S = C.S
    with Stage(C) as st:
        kt, ktb = st.tile([64, 2, NT], BF16, "kt")
        va, vab = st.tile([128, NTILE, 2, 65], BF16, "vaug")
        S.op("gpsimd", "memset", va[:], 1.0, W=[vab])
        vsrc = SC["vtok"].rearrange("(t p) (g d) -> p t g d", p=128, g=2)
        qg = [st.tile([64, 4, NT], BF16, "qg") for _ in range(2)]
        for g in range(2):
            S.dma("sync", kt[:, g, :], SC["kT"][g * 64:(g + 1) * 64, :], reads=[C.scrb], writes=[ktb])
            S.dma("gpsimd", va[:, :, g, 0:64], vsrc[:, :, g, :], reads=[C.scrb], writes=[vab])
            for j in range(4):
                h = 4 * g + j
                S.dma("sync" if j % 2 else "gpsimd", qg[g][0][:, j, :], SC["qT"][h * 64:(h + 1) * 64, :], reads=[C.scrb], writes=[qg[g][1]])
        es, esb = st.tile([128, 8], F32, "esink")
        S.dma("sync", es[:], I["sinkR"][:, l * 8:(l + 1) * 8], writes=[esb])
        S.op("scalar", "activation", out=es[:], in_=es[:], func=ACT.Exp, R=[esb], W=[esb])
        ebufs = [st.tile([128, 512], BF16, "E") for _ in range(6)]
        ei = [0]
        osb = [st.tile([128, 256], F32, "osb") for _ in range(2)]
        den = [st.tile([128, 4], F32, "den") for _ in range(2)]
        stg = Stager(st, [128, 2, 128], BF16, 3, "astg")
        it = 0
        qtiles = ([0, 1] if need_ctx else []) + list(range(2, NTILE))
        for ti in qtiles:
            if ti < 2:
                keys = [(0, None), (1, None)]
            else:
                keys = [(0, None), (1, None)]
                if ti > 2:
                    keys.append((ti - 1, 0))
                keys.append((ti, None))
                if ti < NTILE - 1:
                    keys.append((ti + 1, 1))
            for g in range(2):
                it += 1
                qt, qtb = qg[g]
                es_l = []
                for idx, (kti, mk) in enumerate(keys):
                    pst, psb = C.ps()
                    S.op("tensor", "matmul", pst[:, :].rearrange("p (j t) -> p j t", j=4), lhsT=kt[:, g, kti * 128:(kti + 1) * 128],
                         rhs=qt[:, :, ti * 128:(ti + 1) * 128], start=True, stop=True, R=[ktb, qtb], W=[psb])
                    e, eb = ebufs[ei[0] % 6]
                    ei[0] += 1
                    S.op("scalar", "activation", out=e[:], in_=pst[:, :], func=ACT.Exp, scale=0.125, R=[psb], W=[eb])
                    if mk is not None:
                        S.op("gpsimd" if mk else "vector", "tensor_tensor", out=e[:], in0=e[:], in1=C.amask[:, mk, :], op=ALU.mult,
                             R=[eb, C.constb], W=[eb])
                    es_l.append((e, eb, kti))
                pso, psob = C.ps()
                for j in range(4):
                    for idx, (e, eb, kti) in enumerate(es_l):
                        S.op("tensor", "matmul", pso[:, j * 65:(j + 1) * 65], lhsT=e[:, j * 128:(j + 1) * 128], rhs=va[:, kti, g, :],
                             start=(idx == 0), stop=(idx == len(es_l) - 1), R=[eb, vab], W=[psob])
                dn, dnb = den[it % 2]
                S.op("vector", "tensor_tensor", out=dn[:], in0=pso[:, 0:260].rearrange("p (j c) -> p j c", c=65)[:, :, 64],
                     in1=es[:, 4 * g:4 * g + 4], op=ALU.add, R=[psob, esb], W=[dnb])
                S.op("vector", "reciprocal", out=dn[:], in_=dn[:], R=[dnb], W=[dnb])
                o, ob = osb[it % 2]
                for j in range(4):
                    if j % 2:
                        S.op("vector", "tensor_scalar", out=o[:, j * 64:(j + 1) * 64], in0=pso[:, j * 65:j * 65 + 64], scalar1=dn[:, j:j + 1],
                             scalar2=None, op0=ALU.mult, R=[psob, dnb], W=[ob])
                    else:
                        S.op("scalar", "activation", out=o[:, j * 64:(j + 1) * 64], in_=pso[:, j * 65:j * 65 + 64], func=ACT.Copy,
                             scale=dn[:, j:j + 1], R=[psob, dnb], W=[ob])
                sg, sgb = stg.get()
                for c in range(2):
                    pt, ptb = C.ps()
                    S.op("tensor", "transpose", pt[:, 0:128], o[:, c * 128:(c + 1) * 128], C.ident[:], R=[ob, C.constb], W=[ptb])
                    if c:
                        S.op("vector", "tensor_copy", out=sg[:, c, :], in_=pt[:, 0:128], R=[ptb], W=[sgb])
                    else:
                        S.op("scalar", "copy", out=sg[:, c, :], in_=pt[:, 0:128], R=[ptb], W=[sgb])
                S.dma(C.ldq(), SC["attT"][g * 256:(g + 1) * 256, ti * 128:(ti + 1) * 128].rearrange("(c p) t -> p c t", p=128), sg[:],
                      reads=[sgb], writes=[C.scrb])
def stage_mlstm(C, I, l, SC, need_ctx):
    S = C.S
    with Stage(C) as st:
        gt, gtb = st.tile([128, NTILE, 16], F32, "gt")
        S.dma("sync", gt[:], SC["mlgtok"].rearrange("(t p) c -> p t c", p=128), reads=[C.scrb], writes=[gtb])
        gb, gbb = st.tile([128, 16], F32, "gbias")
        S.dma("sync", gb[:], I["mlgbR"][:, l * 16:(l + 1) * 16], writes=[gbb])
        for t in range(NTILE):
            S.op("vector" if t % 2 else "gpsimd", "tensor_tensor", out=gt[:, t, :], in0=gt[:, t, :], in1=gb[:], op=ALU.add, R=[gtb, gbb], W=[gtb])
        nl, nlb = st.tile([128, NTILE, 16], F32, "nl")
        S.op("scalar", "activation", out=nl[:], in_=gt[:], func=ACT.Exp, scale=-1.0, R=[gtb], W=[nlb])
        S.op("scalar", "activation", out=nl[:], in_=nl[:], func=ACT.Ln, bias=1.0, scale=1.0, R=[nlb], W=[nlb])
        onesf, onesfb = st.tile([128, 128], F32, "onesf")
        S.op("gpsimd", "memset", onesf[:], 1.0, W=[onesfb])
        w_, eb_, ebl_ = [], [], []
        for dd in range(2):
            fc = 4 + 8 * dd
            ic = 8 * dd
            nlc, nlcb = st.tile([128, NTILE, 4], F32, "nlc")
            S.op("vector", "tensor_copy", out=nlc[:], in_=nl[:, :, fc:fc + 4], R=[nlb], W=[nlcb])
            p1, p1b = C.ps()
            S.op("tensor", "matmul", p1[:, 0:NTILE * 4], lhsT=C.tri[:, dd, :], rhs=nlc[:].rearrange("p t c -> p (t c)"), start=True, stop=True,
                 R=[nlcb, C.constb], W=[p1b])
            p2, p2b = C.ps()
            S.op("tensor", "matmul", p2[:, 0:NTILE * 4], lhsT=onesf[:], rhs=nlc[:].rearrange("p t c -> p (t c)"), start=True, stop=True,
                 R=[nlcb, onesfb], W=[p2b])
            w, wb_ = st.tile([128, NTILE, 4], F32, "w")
            eb, ebb = st.tile([128, NTILE, 4], F32, "eb")
            ebl, eblb = st.tile([128, NTILE, 4], F32, "ebl")
            S.op("vector", "tensor_tensor", out=w[:], in0=p1[:, 0:NTILE * 4].rearrange("p (t c) -> p t c", c=4), in1=gt[:, :, ic:ic + 4],
                 op=ALU.add, R=[p1b, gtb], W=[wb_])
            S.op("scalar", "activation", out=w[:], in_=w[:], func=ACT.Exp, R=[wb_], W=[wb_])
            S.op("scalar", "activation", out=eb[:].rearrange("p t c -> p (t c)"), in_=p1[:, 0:NTILE * 4], func=ACT.Exp, scale=-1.0, R=[p1b], W=[ebb])
            S.op("scalar", "activation", out=ebl[:].rearrange("p t c -> p (t c)"), in_=p2[:, 0:NTILE * 4], func=ACT.Exp, scale=-1.0, R=[p2b], W=[eblb])
            w_.append((w, wb_))
            eb_.append((eb, ebb))
            ebl_.append((ebl, eblb))
        gain, gainb = st.tile([128, 512], F32, "gain")
        S.dma("sync", gain[:], I["mlngR"][:, l * 512:(l + 1) * 512], writes=[gainb])

        hb = []
        for i in range(2):
            hb.append(dict(q=st.tile([128, NT], BF16, "mq"), k=st.tile([128, NT], BF16, "mk"),
                           kt=st.tile([128, NTILE, 128], BF16, "mkt"), v=st.tile([128, NTILE, 129], BF16, "mv"),
                           o=st.tile([128, NTILE, 128], F32, "mo"), hf=st.tile([128, NTILE, 128], F32, "hf")))
        Abuf = [st.tile([128, 128], BF16, "A") for _ in range(3)]
        Vw = [st.tile([128, 129], BF16, "Vw") for _ in range(3)]
        dnb_ = [st.tile([128, 2], F32, "dn") for _ in range(3)]
        hs_ = [st.tile([128, 128], F32, "hs") for _ in range(3)]
        sq_ = [st.tile([128, 128], F32, "hsq") for _ in range(2)]
        sg_ = [st.tile([128, 128], F32, "hsg") for _ in range(2)]
        ss_ = [st.tile([128, 2], F32, "ss") for _ in range(3)]
        y_ = [st.tile([128, 128], F32, "y") for _ in range(2)]
        stg = Stager(st, [128, 128], BF16, 3, "mstg")
        Ca, Cab = st.tile([128, 129], F32, "Caug")
        Ct, Ctb = st.tile([128, 129], F32, "Ctmp")
        Cb, Cbb = st.tile([128, 129], BF16, "Cbf")
        it = 0
        for j in range(4):
            H = hb[j % 2]
            q, qb = H["q"]
            k, kb = H["k"]
            ktk, ktkb = H["kt"]
            v, vb = H["v"]
            o, ob = H["o"]
            hf, hfb = H["hf"]
            S.dma("sync", q[:], SC["mlqT"][j * 128:(j + 1) * 128, :], reads=[C.scrb], writes=[qb])
            S.dma("gpsimd", k[:], SC["mlkT"][j * 128:(j + 1) * 128, :], reads=[C.scrb], writes=[kb])
            S.dma("sync", ktk[:], SC["mlktok"].rearrange("(t p) c -> p t c", p=128)[:, :, j * 128:(j + 1) * 128], reads=[C.scrb], writes=[ktkb])
            S.op("gpsimd", "memset", v[:], 1.0, W=[vb])
            S.dma("gpsimd", v[:, :, 0:128], SC["mlvtok"].rearrange("(t p) c -> p t c", p=128)[:, :, j * 128:(j + 1) * 128], reads=[C.scrb], writes=[vb])
            S.dma("sync", o[:], SC["mlotok"].rearrange("(t p) c -> p t c", p=128)[:, :, j * 128:(j + 1) * 128], reads=[C.scrb], writes=[ob])
            for dd in range(2):
                w, wb_ = w_[dd]
                eb, ebb = eb_[dd]
                ebl, eblb = ebl_[dd]
                S.op("gpsimd", "memset", Ca[:], 0.0, W=[Cab])
                S.op("gpsimd", "memset", Cb[:], 0.0, W=[Cbb])
                order = list(range(NTILE)) if dd == 0 else [1, 0] + list(range(NTILE - 1, 1, -1))
                for oi, c in enumerate(order):
                    it += 1
                    cs = slice(c * 128, (c + 1) * 128)
                    pS, pSb = C.ps()
                    S.op("tensor", "matmul", pS[:, 0:128], lhsT=k[:, cs], rhs=q[:, cs], start=True, stop=True, R=[kb, qb], W=[pSb])
                    A, Ab_ = Abuf[it % 3]
                    S.op("vector", "scalar_tensor_tensor", out=A[:], in0=pS[:, 0:128], scalar=w[:, c, j:j + 1], in1=C.tri[:, dd, :],
                         op0=ALU.mult, op1=ALU.mult, R=[pSb, wb_, C.constb], W=[Ab_])
                    vw, vwb = Vw[it % 3]
                    S.op("gpsimd", "tensor_scalar", out=vw[:], in0=v[:, c, :], scalar1=w[:, c, j:j + 1], scalar2=None, op0=ALU.mult,
                         R=[vb, wb_], W=[vwb])
                    pH, pHb = C.ps()
                    S.op("tensor", "matmul", pH[:, 0:129], lhsT=q[:, cs], rhs=Cb[:], start=True, stop=False, R=[qb, Cbb], W=[pHb])
                    S.op("tensor", "matmul", pH[:, 0:129], lhsT=A[:], rhs=v[:, c, :], start=False, stop=True, R=[Ab_, vb], W=[pHb])
                    dn, dnb = dnb_[it % 3]
                    S.op("scalar", "activation", out=dn[:, 0:1], in_=pH[:, 128:129], func=ACT.Abs, scale=eb[:, c, j:j + 1],
                         R=[pHb, ebb], W=[dnb])
                    S.op("vector", "tensor_scalar", out=dn[:, 0:1], in0=dn[:, 0:1], scalar1=1.0, scalar2=None, op0=ALU.max, R=[dnb], W=[dnb])
                    S.op("vector", "reciprocal", out=dn[:, 0:1], in_=dn[:, 0:1], R=[dnb], W=[dnb])
                    S.op("vector", "tensor_tensor", out=dn[:, 1:2], in0=dn[:, 0:1], in1=eb[:, c, j:j + 1], op=ALU.mult, R=[dnb, ebb], W=[dnb])
                    if dd == 0:
                        S.op("scalar", "activation", out=hf[:, c, :], in_=pH[:, 0:128], func=ACT.Copy, scale=dn[:, 1:2], R=[pHb, dnb], W=[hfb])
                    elif c >= 2 or need_ctx:
                        hs, hsb = hs_[it % 3]
                        S.op("vector", "scalar_tensor_tensor", out=hs[:], in0=pH[:, 0:128], scalar=dn[:, 1:2], in1=hf[:, c, :],
                             op0=ALU.mult, op1=ALU.add, R=[pHb, dnb, hfb], W=[hsb])
                        sq, sqb = sq_[it % 2]
                        S.op("gpsimd", "tensor_tensor", out=sq[:], in0=hs[:], in1=hs[:], op=ALU.mult, R=[hsb], W=[sqb])
                        ss, ssb = ss_[it % 3]
                        S.op("vector", "reduce_sum", out=ss[:, 0:1], in_=sq[:], axis=AX.X, R=[sqb], W=[ssb])
                        S.op("scalar", "activation", out=ss[:, 0:1], in_=ss[:, 0:1], func=ACT.Ln, bias=EPS, scale=1.0 / 128, R=[ssb], W=[ssb])
                        S.op("scalar", "activation", out=ss[:, 0:1], in_=ss[:, 0:1], func=ACT.Exp, scale=-0.5, R=[ssb], W=[ssb])
                        sg, sgb = sg_[it % 2]
                        S.op("scalar", "activation", out=sg[:], in_=o[:, c, :], func=ACT.Sigmoid, R=[ob], W=[sgb])
                        y, yb = y_[it % 2]
                        S.op("vector", "scalar_tensor_tensor", out=y[:], in0=hs[:], scalar=ss[:, 0:1], in1=gain[:, j * 128:(j + 1) * 128],
                             op0=ALU.mult, op1=ALU.mult, R=[hsb, ssb, gainb], W=[yb])
                        S.op("gpsimd", "tensor_tensor", out=y[:], in0=y[:], in1=sg[:], op=ALU.mult, R=[yb, sgb], W=[yb])
                        pT, pTb = C.ps()
                        S.op("tensor", "transpose", pT[:, 0:128], y[:], C.ident[:], R=[yb, C.constb], W=[pTb])
                        sb, sbb = stg.get()
                        S.op("scalar", "copy", out=sb[:], in_=pT[:, 0:128], R=[pTb], W=[sbb])
                        S.dma(C.ldq(), SC["mlsT"][j * 128:(j + 1) * 128, cs], sb[:], reads=[sbb], writes=[C.scrb])
                    if oi < len(order) - 1:
                        pC, pCb = C.ps()
                        S.op("tensor", "matmul", pC[:, 0:129], lhsT=ktk[:, c, :], rhs=vw[:], start=True, stop=True, R=[ktkb, vwb], W=[pCb])
                        S.op("vector", "tensor_tensor", out=Ct[:], in0=pC[:, 0:129], in1=Ca[:], op=ALU.add, R=[pCb, Cab], W=[Ctb])
                        S.op("vector", "tensor_scalar", out=Ca[:], in0=Ct[:], scalar1=ebl[:, c, j:j + 1], scalar2=None, op0=ALU.mult,
                             R=[Ctb, eblb], W=[Cab])
                        S.op("scalar", "copy", out=Cb[:], in_=Ca[:], R=[Cab], W=[Cbb])


def stage_merge(C, I, l, SC, xsrc, xdst, G, Gb, need_ctx):
    S = C.S
    with Stage(C) as st:
        wp = make_wpool(st, kc=KC, ncols=1024, nbuf=1)
        wbr = []
        for r in range(3):
            t, tb = st.tile([128, 4, 1024], BF16, "wbr")
            wsrc = I["w_branch"][l, r].rearrange("(k p) n -> p k n", p=128)
            wf, wfb = wp["f"][0]
            S.dma("sync", wf[:, 0:2, :], wsrc[:, 0:2, :], writes=[wfb])
            S.dma("gpsimd", wf[:, 2:4, :], wsrc[:, 2:4, :], writes=[wfb])
            S.op("gpsimd", "tensor_copy", out=t[:, 0:2, :], in_=wf[:, 0:2, :], R=[wfb], W=[tb])
            S.op("vector", "tensor_copy", out=t[:, 2:4, :], in_=wf[:, 2:4, :], R=[wfb], W=[tb])
            wbr.append((t, tb))
        wo, wob = load_weight_bf16(C, I["w_out"][l].rearrange("(k p) n -> p k n", p=128), 1024, wp)
        brs = [[st.tile([128, 4, 512], BF16, "br") for _ in range(3)] for _ in range(2)]
        gts = [st.tile([128, 24, 512], BF16, "gts") for _ in range(1)]
        xts = [st.tile([128, KC, 512], F32, "xts") for _ in range(1)]
        ym, ymb = st.tile([128, KC, 512], BF16, "ym")
        acc = [[st.tile([128, 512], F32, "acc") for _ in range(3)] for _ in range(2)]
        xo = [st.tile([128, 512], F32, "xo") for _ in range(3)]
        srcs = (SC["attT"], SC["mlsT"], SC["hyT"])
        xv = xsrc.rearrange("(k p) n -> p k n", p=128)
        xdv = xdst.rearrange("(k p) n -> p k n", p=128)
        it = 0
        for bi, (t0, n) in enumerate(BLOCKS):
            if bi == 0 and not need_ctx:
                continue
            w = 1 if bi == 0 else 0
            br = brs[bi % 2]
            for r in range(3):
                S.dma("sync" if r != 1 else "gpsimd", br[r][0][:, :, 0:n], srcs[r].rearrange("(k p) n -> p k n", p=128)[:, :, t0:t0 + n],
                      reads=[C.scrb], writes=[br[r][1]])
            g, gbf = gts[0]
            S.dma("sync", g[:, 0:12, 0:n], SC["bgT"].rearrange("(k p) n -> p k n", p=128)[:, 0:12, t0:t0 + n], reads=[C.scrb], writes=[gbf])
            S.dma("gpsimd", g[:, 12:24, 0:n], SC["bgT"].rearrange("(k p) n -> p k n", p=128)[:, 12:24, t0:t0 + n], reads=[C.scrb], writes=[gbf])
            xt, xtb = xts[0]
            S.dma("sync", xt[:, :, 0:n], xv[:, :, t0:t0 + n], reads=[C.scrb], writes=[xtb])
            for oc in range(KC):
                it += 1
                a = acc[it % 2]
                for r in range(3):
                    pst, psb = C.ps()
                    for k in range(4):
                        S.op("tensor", "matmul", pst[:, 0:n], lhsT=wbr[r][0][:, k, oc * 128:(oc + 1) * 128], rhs=br[r][0][:, k, 0:n],
                             start=(k == 0), stop=(k == 3), R=[wbr[r][1], br[r][1]], W=[psb])
                    S.op("vector", "tensor_tensor", out=a[r][0][:, 0:n], in0=pst[:, 0:n], in1=g[:, r * 8 + oc, 0:n], op=ALU.mult,
                         R=[psb, gbf], W=[a[r][1]])
                S.op("gpsimd", "tensor_tensor", out=a[0][0][:, 0:n], in0=a[0][0][:, 0:n], in1=a[1][0][:, 0:n], op=ALU.add,
                     R=[a[0][1], a[1][1]], W=[a[0][1]])
                S.op("gpsimd", "tensor_tensor", out=ym[:, oc, 0:n], in0=a[0][0][:, 0:n], in1=a[2][0][:, 0:n], op=ALU.add,
                     R=[a[0][1], a[2][1]], W=[ymb])
            for oc in range(KC):
                pst, psb = C.ps()
                for k in range(KC):
                    S.op("tensor", "matmul", pst[:, 0:n], lhsT=wo[:, k, oc * 128:(oc + 1) * 128], rhs=ym[:, k, 0:n],
                         start=(k == 0), stop=(k == KC - 1), R=[wob, ymb], W=[psb])
                it += 1
                o, ob = xo[it % 3]
                S.op("vector", "scalar_tensor_tensor", out=o[:, 0:n], in0=pst[:, 0:n], scalar=G[:, oc, w:w + 1], in1=xt[:, oc, 0:n],
                     op0=ALU.mult, op1=ALU.add, R=[psb, Gb, xtb], W=[ob])
                S.dma(C.ldq(), xdv[:, oc, t0:t0 + n], o[:, 0:n], reads=[ob], writes=[C.scrb])


def stage_ffn_up(C, I, l, SC, xs, A, B, Ab, need_ctx):
    S = C.S
    blocks = [(bi, b) for bi, b in enumerate(BLOCKS) if bi > 0 or need_ctx]
    with Stage(C) as st:
        hT, hTb = st.tile([128, KC, NT], BF16, "hT2")
        norm_mod(C, st, xs, hT, hTb, A, B, Ab, blocks)
        wpool = make_wpool(st, ncols=256, nbuf=1)
        wv = I["w_up"][l].rearrange("(k p) n -> p k n", p=128)
        cw, cwb = st.tile([128, 3, 44], F32, "cw")
        S.dma("sync", cw[:], I["fcwT"][:, l, :, :], writes=[cwb])
        cbias, cbb = st.tile([128, 44], F32, "cbias")
        S.dma("sync", cbias[:], I["fcbT"][:, l, :], writes=[cbb])
        ug, ugb = st.tile([128, L + 2], F32, "ug")
        uv, uvb = st.tile([128, L + 2], F32, "uv")
        cg, cgb = st.tile([128, L], F32, "cg")
        cv, cvb = st.tile([128, L], F32, "cv")
        ab = [st.tile([128, L], BF16, "abf") for _ in range(1)]
        it = 0
        for j in range(22):
            i = wpool["i"]
            wpool["i"] += 1
            wf, wfb = wpool["f"][0]
            wb, wbb = wpool["b"][0]
            S.dma("sync", wf[:, :, 0:128], wv[:, :, j * 128:(j + 1) * 128], writes=[wfb])
            S.dma("gpsimd", wf[:, :, 128:256], wv[:, :, DFF + j * 128:DFF + (j + 1) * 128], writes=[wfb])
            S.op("gpsimd", "tensor_copy", out=wb[:, :, 0:128], in_=wf[:, :, 0:128], R=[wfb], W=[wbb])
            S.op("vector", "tensor_copy", out=wb[:, :, 128:256], in_=wf[:, :, 128:256], R=[wfb], W=[wbb])
            for seq in ((0, LC), (LC, L)):
                s0, ns = seq
                if s0 == 0 and not need_ctx:
                    continue
                it += 1
                S.op("gpsimd", "memset", ug[:, 0:1], 0.0, W=[ugb])
                S.op("gpsimd", "memset", ug[:, ns + 1:ns + 2], 0.0, W=[ugb])
                S.op("gpsimd", "memset", uv[:, 0:1], 0.0, W=[uvb])
                S.op("gpsimd", "memset", uv[:, ns + 1:ns + 2], 0.0, W=[uvb])
                for bi, (t0, n) in blocks:
                    if not (s0 <= t0 < s0 + ns):
                        continue
                    off = t0 - s0
                    for half, (dst, dstb) in enumerate(((ug, ugb), (uv, uvb))):
                        pst, psb = C.ps()
                        for k in range(KC):
                            S.op("tensor", "matmul", pst[:, 0:n], lhsT=wb[:, k, half * 128:(half + 1) * 128], rhs=hT[:, k, t0:t0 + n],
                                 start=(k == 0), stop=(k == KC - 1), R=[wbb, hTb], W=[psb])
                        if half:
                            S.op("vector", "tensor_copy", out=dst[:, 1 + off:1 + off + n], in_=pst[:, 0:n], R=[psb], W=[dstb])
                        else:
                            S.op("scalar", "copy", out=dst[:, 1 + off:1 + off + n], in_=pst[:, 0:n], R=[psb], W=[dstb])
                for half, (src, srcb, dst, dstb) in enumerate(((ug, ugb, cg, cgb), (uv, uvb, cv, cvb))):
                    f = j + 22 * half
                    S.op("scalar", "activation", out=dst[:, 0:ns], in_=src[:, 0:ns], func=ACT.Identity, scale=cw[:, 0, f:f + 1],
                         bias=cbias[:, f:f + 1], R=[srcb, cwb, cbb], W=[dstb])
                    S.op("vector", "scalar_tensor_tensor", out=dst[:, 0:ns], in0=src[:, 1:ns + 1], scalar=cw[:, 1, f:f + 1], in1=dst[:, 0:ns],
                         op0=ALU.mult, op1=ALU.add, R=[srcb, cwb, dstb], W=[dstb])
                    S.op("vector", "scalar_tensor_tensor", out=dst[:, 0:ns], in0=src[:, 2:ns + 2], scalar=cw[:, 2, f:f + 1], in1=dst[:, 0:ns],
                         op0=ALU.mult, op1=ALU.add, R=[srcb, cwb, dstb], W=[dstb])
                S.op("scalar", "activation", out=cg[:, 0:ns], in_=cg[:, 0:ns], func=ACT.Silu, R=[cgb], W=[cgb])
                a, abb = ab[0]
                S.op("gpsimd", "tensor_tensor", out=a[:, 0:ns], in0=cg[:, 0:ns], in1=cv[:, 0:ns], op=ALU.mult, R=[cgb, cvb], W=[abb])
                S.dma(C.ldq(), SC["aT"][j * 128:(j + 1) * 128, s0:s0 + ns], a[:, 0:ns], reads=[abb], writes=[C.scrb])


def stage_ffn_down(C, I, l, SC, xs, G, Gb, need_ctx):
    S = C.S
    with Stage(C) as st:
        wd, wdb = st.tile([128, 22, 1024], BF16, "wd")
        wf = [st.tile([128, 11, 512], F32, "wdf") for _ in range(2)]
        wsrc = I["w_down"][l].rearrange("(k p) n -> p k n", p=128)
        i = 0
        for kh in range(2):
            for ch in range(2):
                f, fb = wf[i % 2]
                i += 1
                S.dma("sync" if i % 2 else "gpsimd", f[:], wsrc[:, kh * 11:(kh + 1) * 11, ch * 512:(ch + 1) * 512], writes=[fb])
                S.op("gpsimd" if i % 2 else "vector", "tensor_copy", out=wd[:, kh * 11:(kh + 1) * 11, ch * 512:(ch + 1) * 512], in_=f[:],
                     R=[fb], W=[wdb])
        ats = [st.tile([128, 22, 512], BF16, "at") for _ in range(2)]
        xts = [st.tile([128, KC, 512], F32, "xts") for _ in range(2)]
        xo = [st.tile([128, 512], F32, "xo") for _ in range(3)]
        xv = xs.rearrange("(k p) n -> p k n", p=128)
        av = SC["aT"].rearrange("(k p) n -> p k n", p=128)
        it = 0
        for bi, (t0, n) in enumerate(BLOCKS):
            if bi == 0 and not need_ctx:
                continue
            w = 1 if bi == 0 else 0
            a, ab_ = ats[bi % 2]
            S.dma("sync", a[:, 0:11, 0:n], av[:, 0:11, t0:t0 + n], reads=[C.scrb], writes=[ab_])
            S.dma("gpsimd", a[:, 11:22, 0:n], av[:, 11:22, t0:t0 + n], reads=[C.scrb], writes=[ab_])
            xt, xtb = xts[bi % 2]
            S.dma("sync", xt[:, :, 0:n], xv[:, :, t0:t0 + n], reads=[C.scrb], writes=[xtb])
            for oc in range(KC):
                pst, psb = C.ps()
                for k in range(22):
                    S.op("tensor", "matmul", pst[:, 0:n], lhsT=wd[:, k, oc * 128:(oc + 1) * 128], rhs=a[:, k, 0:n],
                         start=(k == 0), stop=(k == 21), R=[wdb, ab_], W=[psb])
                it += 1
                o, ob = xo[it % 3]
                S.op("vector", "scalar_tensor_tensor", out=o[:, 0:n], in0=pst[:, 0:n], scalar=G[:, oc, w:w + 1], in1=xt[:, oc, 0:n],
                     op0=ALU.mult, op1=ALU.add, R=[psb, Gb, xtb], W=[ob])
                S.dma(C.ldq(), xv[:, oc, t0:t0 + n], o[:, 0:n], reads=[ob], writes=[C.scrb])


def stage_final(C, I, xs, outT, fg, fgb):
    S = C.S
    with Stage(C) as st:
        xb = [st.tile([128, KC, 512], F32, "xblk") for _ in range(2)]
        sqs = [st.tile([128, 512], BF16, "sq") for _ in range(2)]
        rs, rsb = st.tile([128, 512], F32, "rstd")
        ob_ = [st.tile([128, KC, 512], F32, "oblk") for _ in range(2)]
        xv = xs.rearrange("(k p) n -> p k n", p=128)
        ov = outT.rearrange("(k p) n -> p k n", p=128)
        for bi, (t0, n) in enumerate(BLOCKS):
            if bi == 0:
                continue
            xt, xtb = xb[bi % 2]
            S.dma("sync", xt[:, 0:4, 0:n], xv[:, 0:4, t0:t0 + n], reads=[C.scrb], writes=[xtb])
            S.dma("gpsimd", xt[:, 4:8, 0:n], xv[:, 4:8, t0:t0 + n], reads=[C.scrb], writes=[xtb])
            rms_rstd(C, st, xt, xtb, n, rs, rsb, sqs)
            o, ob = ob_[bi % 2]
            for k in range(KC):
                S.op("vector", "scalar_tensor_tensor", out=o[:, k, 0:n], in0=xt[:, k, 0:n], scalar=fg[:, k:k + 1],
                     in1=rs[:, 0:n], op0=ALU.mult, op1=ALU.mult, R=[xtb, rsb, fgb], W=[ob])
            S.dma("sync", ov[:, 0:4, t0 - LC:t0 - LC + n], o[:, 0:4, 0:n], reads=[ob], writes=[C.outb])
            S.dma("gpsimd", ov[:, 4:8, t0 - LC:t0 - LC + n], o[:, 4:8, 0:n], reads=[ob], writes=[C.outb])
HSEQ = {
    "lat": dict(Ls=L, col0=LC, NTs=L // 128, TB=256, NBt=L // 256),
    "ctx": dict(Ls=LC, col0=0, NTs=LC // 128, TB=256, NBt=1),
}
TWO_PI = 2.0 * math.pi


def hyena_inputs(inp):
    for s, q in HSEQ.items():
        inp("hyF_" + s, [q["NTs"], 2, 128, q["NTs"], 128], BF16)
        inp("hyG_" + s, [q["NBt"], 2, 128, q["NTs"], q["TB"]], BF16)
        inp("hyz_" + s, [33, q["Ls"]])
        inp("hytn_" + s, [128, q["NTs"]])
    inp("hydel", [128, 512])
    inp("hy_w1", [DEPTH, 33, 64])
    inp("hy_w2", [DEPTH, 64, 64])
    inp("hy_w3", [DEPTH, 64, 2048])
    inp("hy_fb", [64, DEPTH, 4])
    inp("hy_skipT", [128, DEPTH, 2, 4])
    inp("hy_swT", [128, DEPTH, 3, 12])
    inp("hy_sbT", [128, DEPTH, 12])


def hyena_scratch(scr):
    scr("hyc", [1536, NT], F32)
    scr("hfil", [L, 2048], BF16)
    scr("hspec", [4, 2, L, 512], F32)
    scr("hz", [512, NT], F32)


def hyena_host(inputs, m):
    cst = _consts()
    if "hy" not in cst:
        hy = {}
        for s, q in HSEQ.items():
            Ls = q["Ls"]
            n = 2 * Ls
            t = np.arange(Ls, dtype=np.float64)
            f = np.arange(Ls, dtype=np.float64)
            ang = (TWO_PI / n) * np.outer(t, f + 0.5)
            NTs, TB, NBt = q["NTs"], q["TB"], q["NBt"]
            Fc = np.cos(ang).reshape(NTs, 128, NTs, 128).transpose(2, 1, 0, 3)
            Fs = np.sin(ang).reshape(NTs, 128, NTs, 128).transpose(2, 1, 0, 3)
            hy["hyF_" + s] = np.ascontiguousarray(np.stack([Fc, Fs], axis=1)).astype(ml_dtypes.bfloat16)
            angT = ang.T
            Gc = ((2.0 / n) * np.cos(angT)).reshape(NTs, 128, NBt, TB).transpose(2, 1, 0, 3)
            Gs = (-(2.0 / n) * np.sin(angT)).reshape(NTs, 128, NBt, TB).transpose(2, 1, 0, 3)
            hy["hyG_" + s] = np.ascontiguousarray(np.stack([Gc, Gs], axis=1)).astype(ml_dtypes.bfloat16)
            t32 = np.arange(Ls, dtype=np.float32)
            tn = (t32 / np.float32(max(Ls - 1, 1))).astype(np.float32)
            w = (np.float32(TWO_PI) * t32 / np.float32(Ls)).astype(np.float32)
            bands = np.linspace(1e-4, 15, 16, dtype=np.float32)
            z = np.concatenate([tn[:, None], np.cos(w[:, None] * bands), -np.sin(w[:, None] * bands)], axis=-1).astype(np.float32)
            hy["hyz_" + s] = np.ascontiguousarray(z.T)
            hy["hytn_" + s] = np.ascontiguousarray((-tn).reshape(NTs, 128).T)
        deltas = np.abs(np.linspace(math.log(1e-2) / 1.5, math.log(1e-2) / 0.3, 512, dtype=np.float32)).astype(np.float32)
        hy["hydel"] = _rep(deltas)
        cst["hy"] = hy
    m.update(cst["hy"])
    for k in ("hy_w1", "hy_w2", "hy_w3"):
        m[k] = np.ascontiguousarray(inputs[k], dtype=np.float32)
    fr = np.asarray(inputs["hy_freq"], np.float32)
    fb = np.stack([fr[:, 0], np.asarray(inputs["hy_b1"], np.float32), fr[:, 1], np.asarray(inputs["hy_b2"], np.float32)], axis=-1)
    m["hy_fb"] = np.ascontiguousarray(fb.transpose(1, 0, 2))
    m["hy_skipT"] = _pk(inputs["hy_skip"])
    m["hy_swT"] = _pk(inputs["hy_short_w"])
    m["hy_sbT"] = _pk(inputs["hy_short_b"])


def hy_fbufs(st, s):
    return [[st.tile([128, HSEQ[s]["NTs"], 128], BF16, "Ft") for _ in range(2)] for _ in range(2)]


def hy_fwd(C, Fb, I, s, Xtok, Xb, consume):
    S = C.S
    q = HSEQ[s]
    NTs = q["NTs"]
    for fc in range(NTs):
        ft = Fb[fc % 2]
        S.dma("sync", ft[0][0][:], I["hyF_" + s][fc, 0], writes=[ft[0][1]])
        S.dma("gpsimd", ft[1][0][:], I["hyF_" + s][fc, 1], writes=[ft[1][1]])
        pp = []
        for part in range(2):
            pst, psb = C.ps()
            for k in range(NTs):
                S.op("tensor", "matmul", pst[:, :], lhsT=ft[part][0][:, k, :], rhs=Xtok[:, k, :], start=(k == 0), stop=(k == NTs - 1),
                     R=[ft[part][1], Xb], W=[psb])
            pp.append((pst, psb))
        consume(fc, pp[0][0], pp[0][1], pp[1][0], pp[1][1])


def stage_hyena(C, I, l, SC, need_ctx):
    S = C.S
    with Stage(C) as st:
        sw, swb = st.tile([128, 3, 12], F32, "sw")
        S.dma("sync", sw[:], I["hy_swT"][:, l, :, :], writes=[swb])
        sbi, sbib = st.tile([128, 12], F32, "sbi")
        S.dma("sync", sbi[:], I["hy_sbT"][:, l, :], writes=[sbib])
        ub = [st.tile([128, L + 2], F32, "hu") for _ in range(2)]
        cb = [st.tile([128, L], F32, "hc") for _ in range(2)]
        it = 0
        for ch in range(12):
            for (s0, ns) in ((0, LC), (LC, L)):
                it += 1
                u, ubb = ub[it % 2]
                c, cbb = cb[it % 2]
                S.op("gpsimd", "memset", u[:, 0:1], 0.0, W=[ubb])
                S.op("gpsimd", "memset", u[:, ns + 1:ns + 2], 0.0, W=[ubb])
                S.dma("sync" if it % 2 else "gpsimd", u[:, 1:ns + 1], SC["hyu"][ch * 128:(ch + 1) * 128, s0:s0 + ns], writes=[ubb])
                S.op("scalar", "activation", out=c[:, 0:ns], in_=u[:, 0:ns], func=ACT.Identity, scale=sw[:, 0, ch:ch + 1],
                     bias=sbi[:, ch:ch + 1], R=[ubb, swb, sbib], W=[cbb])
                S.op("vector", "scalar_tensor_tensor", out=c[:, 0:ns], in0=u[:, 1:ns + 1], scalar=sw[:, 1, ch:ch + 1], in1=c[:, 0:ns],
                     op0=ALU.mult, op1=ALU.add, R=[ubb, swb, cbb], W=[cbb])
                S.op("vector", "scalar_tensor_tensor", out=c[:, 0:ns], in0=u[:, 2:ns + 2], scalar=sw[:, 2, ch:ch + 1], in1=c[:, 0:ns],
                     op0=ALU.mult, op1=ALU.add, R=[ubb, swb, cbb], W=[cbb])
                S.dma("sync" if it % 2 else "gpsimd", SC["hyc"][ch * 128:(ch + 1) * 128, s0:s0 + ns], c[:, 0:ns], reads=[cbb])

    for s in (("lat", "ctx") if need_ctx else ("lat",)):
        q = HSEQ[s]
        Ls, col0, NTs, TB, NBt = q["Ls"], q["col0"], q["NTs"], q["TB"], q["NBt"]
        with Stage(C) as so:
            rn, rnb = so.tile([128, 2048], F32, "rn")
            skp, skpb = so.tile([128, 2, 4], F32, "skp")
            S.dma("sync", skp[:], I["hy_skipT"][:, l, :, :], writes=[skpb])
            with Stage(C) as st:
                zT, zTb = st.tile([33, Ls], F32, "zT")
                S.dma("sync", zT[:], I["hyz_" + s][:, :], writes=[zTb])
                w1, w1b = st.tile([33, 64], F32, "w1")
                S.dma("sync", w1[:], I["hy_w1"][l], writes=[w1b])
                w2, w2b = st.tile([64, 64], F32, "w2")
                S.dma("sync", w2[:], I["hy_w2"][l], writes=[w2b])
                w3, w3b = st.tile([64, 2048], F32, "w3")
                S.dma("gpsimd", w3[:], I["hy_w3"][l], writes=[w3b])
                fb, fbb = st.tile([64, 6], F32, "fb")
                S.dma("sync", fb[:, 0:4], I["hy_fb"][:, l, :], writes=[fbb])
                S.op("vector", "tensor_tensor", out=fb[:, 4:5], in0=fb[:, 0:1], in1=fb[:, 1:2], op=ALU.mult, R=[fbb], W=[fbb])
                S.op("vector", "tensor_tensor", out=fb[:, 5:6], in0=fb[:, 2:3], in1=fb[:, 3:4], op=ALU.mult, R=[fbb], W=[fbb])
                tn, tnb = st.tile([128, NTs], F32, "tn")
                S.dma("sync", tn[:], I["hytn_" + s][:, :], writes=[tnb])
                dl, dlb = st.tile([128, 512], F32, "del")
                S.dma("sync", dl[:], I["hydel"][:, :], writes=[dlb])
                h1, h1b = st.tile([64, 512], F32, "h1")
                h2, h2b = st.tile([64, Ls], F32, "h2")
                ta = [st.tile([64, 512], F32, "ta") for _ in range(2)]
                tm_ = [st.tile([64, 512], F32, "tm") for _ in range(2)]
                nb_ = min(512, Ls)
                for bi in range(Ls // nb_):
                    bs = slice(bi * nb_, (bi + 1) * nb_)
                    for (wt, wtb, src, srcb, fcol, bcol, dst, dstb, dsl) in (
                            (w1, w1b, zT[:, bs], zTb, 0, 4, h1, h1b, slice(0, nb_)),
                            (w2, w2b, h1[:, 0:nb_], h1b, 2, 5, h2, h2b, bs)):
                        pst, psb = C.ps()
                        S.op("tensor", "matmul", pst[0:64, 0:nb_], lhsT=wt[:], rhs=src, start=True, stop=True, R=[wtb, srcb], W=[psb])
                        a, ab_ = ta[bi % 2]
                        S.op("vector", "tensor_scalar", out=a[:, 0:nb_], in0=pst[0:64, 0:nb_], scalar1=fb[:, fcol:fcol + 1],
                             scalar2=fb[:, bcol:bcol + 1], op0=ALU.mult, op1=ALU.add, R=[psb, fbb], W=[ab_])
                        m_, mb_ = tm_[bi % 2]
                        for _r in range(2):
                            S.op("vector", "tensor_scalar", out=m_[:, 0:nb_], in0=a[:, 0:nb_], scalar1=math.pi, scalar2=-TWO_PI,
                                 op0=ALU.is_gt, op1=ALU.mult, R=[ab_], W=[mb_])
                            S.op("vector", "tensor_tensor", out=a[:, 0:nb_], in0=a[:, 0:nb_], in1=m_[:, 0:nb_], op=ALU.add, R=[ab_, mb_], W=[ab_])
                            S.op("vector", "tensor_scalar", out=m_[:, 0:nb_], in0=a[:, 0:nb_], scalar1=-math.pi, scalar2=TWO_PI,
                                 op0=ALU.is_lt, op1=ALU.mult, R=[ab_], W=[mb_])
                            S.op("vector", "tensor_tensor", out=a[:, 0:nb_], in0=a[:, 0:nb_], in1=m_[:, 0:nb_], op=ALU.add, R=[ab_, mb_], W=[ab_])
                        S.op("vector", "tensor_scalar", out=a[:, 0:nb_], in0=a[:, 0:nb_], scalar1=-3.1415925, scalar2=3.1415925,
                             op0=ALU.max, op1=ALU.min, R=[ab_], W=[ab_])
                        S.op("scalar", "activation", out=dst[:, dsl], in_=a[:, 0:nb_], func=ACT.Sin, R=[ab_], W=[dstb])
                dec = [st.tile([128, 512], F32, "dec") for _ in range(2)]
                hd = [st.tile([128, 512], BF16, "hd") for _ in range(3)]
                ha = [st.tile([128, 512], BF16, "ha") for _ in range(3)]
                nbank = [4, 5, 6, 7]
                C.reserved = set(nbank)
                it = 0
                for k in range(NTs):
                    d, db = dec[k % 2]
                    S.op("scalar", "activation", out=d[:], in_=dl[:], func=ACT.Exp, scale=tn[:, k:k + 1], R=[dlb, tnb], W=[db])
                    for g in range(4):
                        it += 1
                        pst, psb = C.ps()
                        S.op("tensor", "matmul", pst[:, :], lhsT=h2[:, k * 128:(k + 1) * 128], rhs=w3[:, g * 512:(g + 1) * 512], start=True, stop=True,
                             R=[h2b, w3b], W=[psb])
                        hh, hhb = hd[it % 3]
                        S.op("vector", "tensor_tensor", out=hh[:], in0=pst[:, :], in1=d[:], op=ALU.mult, R=[psb, db], W=[hhb])
                        aa, aab = ha[it % 3]
                        S.op("scalar", "activation", out=aa[:], in_=hh[:], func=ACT.Abs, R=[hhb], W=[aab])
                        S.op("tensor", "matmul", C.psum[nbank[g]][:, :], lhsT=C.ones_bf[:], rhs=aa[:], start=(k == 0), stop=(k == NTs - 1),
                             R=[aab, C.constb], W=[C.psb[nbank[g]]])
                        S.dma(C.ldq(), SC["hfil"][k * 128:(k + 1) * 128, g * 512:(g + 1) * 512], hh[:], reads=[hhb])
                for g in range(4):
                    S.op("vector", "reciprocal", out=rn[:, g * 512:(g + 1) * 512], in_=C.psum[nbank[g]][:, :], R=[C.psb[nbank[g]]], W=[rnb])
                C.reserved = set()
            with Stage(C) as st:
                Xb_ = [st.tile([128, NTs, 512], BF16, "Xf") for _ in range(2)]
                stg = Stager(st, [128, 512], F32, 4, "spst")
                Fb2 = hy_fbufs(st, s)
                for g in range(4):
                    X, Xb = Xb_[g % 2]
                    S.dma("sync", X[:, 0:NTs // 2, :], SC["hfil"][0:Ls, g * 512:(g + 1) * 512].rearrange("(k p) c -> p k c", p=128)[:, 0:NTs // 2, :], writes=[Xb])
                    S.dma("gpsimd", X[:, NTs // 2:NTs, :], SC["hfil"][0:Ls, g * 512:(g + 1) * 512].rearrange("(k p) c -> p k c", p=128)[:, NTs // 2:NTs, :], writes=[Xb])

                    def consume(fc, pA, pAb, pB, pBb, g=g):
                        for part, (p_, pb_) in enumerate(((pA, pAb), (pB, pBb))):
                            sb, sbb = stg.get()
                            if part:
                                S.op("vector", "tensor_copy", out=sb[:], in_=p_[:, :], R=[pb_], W=[sbb])
                            else:
                                S.op("scalar", "copy", out=sb[:], in_=p_[:, :], R=[pb_], W=[sbb])
                            S.dma(C.ldq(), SC["hspec"][g, part, fc * 128:(fc + 1) * 128, :], sb[:], reads=[sbb])
                    hy_fwd(C, Fb2, I, s, X, Xb, consume)
            for o in range(2):
                with Stage(C) as sz:
                    Zr, Zrb = sz.tile([128, NTs, 512], BF16, "Zr")
                    Zi, Zib = sz.tile([128, NTs, 512], BF16, "Zi")
                    usrc = SC["hyc"][1024:1536, :] if o == 0 else SC["hz"]
                    gsrc = SC["hyc"][o * 512:(o + 1) * 512, :]
                    with Stage(C) as st:
                        X, Xb = st.tile([128, NTs, 512], BF16, "Xd")
                        UW = min(1024, Ls)
                        uf = [st.tile([128, UW], F32, "uf") for _ in range(2)]
                        ui = 0
                        for cc in range(4):
                            for u0 in range(0, Ls, UW):
                                ui += 1
                                u, ub_ = uf[ui % 2]
                                S.dma("sync" if ui % 2 else "gpsimd", u[:], usrc[cc * 128:(cc + 1) * 128, col0 + u0:col0 + u0 + UW], writes=[ub_])
                                for k0 in range(0, UW // 128, 4):
                                    kn = min(4, UW // 128 - k0)
                                    kb0 = u0 // 128 + k0
                                    pt, ptb = C.ps()
                                    for kk in range(kn):
                                        S.op("tensor", "transpose", pt[:, kk * 128:(kk + 1) * 128], u[:, (k0 + kk) * 128:(k0 + kk + 1) * 128], C.ident[:],
                                             R=[ub_, C.constb], W=[ptb])
                                    if (k0 // 4) % 2:
                                        S.op("vector", "tensor_copy", out=X[:, kb0:kb0 + kn, cc * 128:(cc + 1) * 128],
                                             in_=pt[:, 0:kn * 128].rearrange("p (k c) -> p k c", c=128), R=[ptb], W=[Xb])
                                    else:
                                        S.op("scalar", "copy", out=X[:, kb0:kb0 + kn, cc * 128:(cc + 1) * 128],
                                             in_=pt[:, 0:kn * 128].rearrange("p (k c) -> p k c", c=128), R=[ptb], W=[Xb])
                        sp = [[st.tile([128, 512], F32, "sp") for _ in range(4)] for _ in range(1)]
                        Kr_ = [st.tile([128, 512], F32, "Kr") for _ in range(1)]
                        Ki_ = [st.tile([128, 512], F32, "Ki") for _ in range(1)]
                        tt = [st.tile([128, 512], F32, "tt") for _ in range(4)]
                        rf = rn[:, o * 1024:o * 1024 + 512]
                        rb = rn[:, o * 1024 + 512:o * 1024 + 1024]

                        def consume(fc, pA, pAb, pB, pBb, o=o):
                            spt = sp[0]
                            fs = slice(fc * 128, (fc + 1) * 128)
                            for i_, (g_, part) in enumerate(((2 * o, 0), (2 * o, 1), (2 * o + 1, 0), (2 * o + 1, 1))):
                                S.dma("sync" if i_ % 2 else "gpsimd", spt[i_][0][:], SC["hspec"][g_, part, fs, :], writes=[spt[i_][1]])
                            Kr, Krb = Kr_[0]
                            Ki, Kib = Ki_[0]
                            (Af, Afb), (Bf, Bfb), (Ab, Abb), (Bb, Bbb) = spt
                            S.op("gpsimd", "tensor_tensor", out=Af[:], in0=Af[:], in1=rf, op=ALU.mult, R=[Afb, rnb], W=[Afb])
                            S.op("gpsimd", "tensor_tensor", out=Ab[:], in0=Ab[:], in1=rb, op=ALU.mult, R=[Abb, rnb], W=[Abb])
                            S.op("gpsimd", "tensor_tensor", out=Kr[:], in0=Af[:], in1=Ab[:], op=ALU.add, R=[Afb, Abb], W=[Krb])
                            S.op("gpsimd", "tensor_tensor", out=Bf[:], in0=Bf[:], in1=rf, op=ALU.mult, R=[Bfb, rnb], W=[Bfb])
                            S.op("gpsimd", "tensor_tensor", out=Bb[:], in0=Bb[:], in1=rb, op=ALU.mult, R=[Bbb, rnb], W=[Bbb])
                            S.op("gpsimd", "tensor_tensor", out=Ki[:], in0=Bb[:], in1=Bf[:], op=ALU.subtract, R=[Bbb, Bfb], W=[Kib])
                            (t1, t1b), (t2, t2b), (t3, t3b), (t4, t4b) = tt
                            S.op("vector", "tensor_tensor", out=t1[:], in0=pA[:, :], in1=Kr[:], op=ALU.mult, R=[pAb, Krb], W=[t1b])
                            S.op("vector", "tensor_tensor", out=t2[:], in0=pB[:, :], in1=Ki[:], op=ALU.mult, R=[pBb, Kib], W=[t2b])
                            S.op("vector", "tensor_tensor", out=t3[:], in0=pA[:, :], in1=Ki[:], op=ALU.mult, R=[pAb, Kib], W=[t3b])
                            S.op("vector", "tensor_tensor", out=t4[:], in0=pB[:, :], in1=Kr[:], op=ALU.mult, R=[pBb, Krb], W=[t4b])
                            S.op("gpsimd", "tensor_tensor", out=Zr[:, fc, :], in0=t1[:], in1=t2[:], op=ALU.add, R=[t1b, t2b], W=[Zrb])
                            S.op("vector", "tensor_tensor", out=Zi[:, fc, :], in0=t3[:], in1=t4[:], op=ALU.subtract, R=[t3b, t4b], W=[Zib])
                        hy_fwd(C, hy_fbufs(st, s), I, s, X, Xb, consume)
                    with Stage(C) as st:
                        Gt = [[st.tile([128, NTs, TB], BF16, "Gt") for _ in range(2)] for _ in range(2)]
                        ut = [st.tile([128, TB], F32, "ut") for _ in range(3)]
                        gt_ = [st.tile([128, TB], F32, "gt") for _ in range(3)]
                        o1 = [st.tile([128, TB], F32, "o1") for _ in range(3)]
                        ob16 = [st.tile([128, TB], BF16, "ob16") for _ in range(3)]
                        it = 0
                        for tb in range(NBt):
                            G = Gt[tb % 2]
                            S.dma("sync", G[0][0][:], I["hyG_" + s][tb, 0], writes=[G[0][1]])
                            S.dma("gpsimd", G[1][0][:], I["hyG_" + s][tb, 1], writes=[G[1][1]])
                            tsl = slice(col0 + tb * TB, col0 + (tb + 1) * TB)
                            for cc in range(4):
                                it += 1
                                u, ub_ = ut[it % 3]
                                gg, ggb = gt_[it % 3]
                                S.dma("sync", u[:], usrc[cc * 128:(cc + 1) * 128, tsl], writes=[ub_])
                                S.dma("gpsimd", gg[:], gsrc[cc * 128:(cc + 1) * 128, tsl], writes=[ggb])
                                pst, psb = C.ps()
                                for fk in range(NTs):
                                    S.op("tensor", "matmul", pst[:, 0:TB], lhsT=Zr[:, fk, cc * 128:(cc + 1) * 128], rhs=G[0][0][:, fk, :],
                                         start=(fk == 0), stop=False, R=[Zrb, G[0][1]], W=[psb])
                                    S.op("tensor", "matmul", pst[:, 0:TB], lhsT=Zi[:, fk, cc * 128:(cc + 1) * 128], rhs=G[1][0][:, fk, :],
                                         start=False, stop=(fk == NTs - 1), R=[Zib, G[1][1]], W=[psb])
                                t1, t1b = o1[it % 3]
                                S.op("vector", "scalar_tensor_tensor", out=t1[:], in0=u[:], scalar=skp[:, o, cc:cc + 1], in1=pst[:, 0:TB],
                                     op0=ALU.mult, op1=ALU.add, R=[ub_, skpb, psb], W=[t1b])
                                if o == 0:
                                    S.op("gpsimd", "tensor_tensor", out=t1[:], in0=t1[:], in1=gg[:], op=ALU.mult, R=[t1b, ggb], W=[t1b])
                                    S.dma(C.ldq(), SC["hz"][cc * 128:(cc + 1) * 128, tsl], t1[:], reads=[t1b])
                                else:
                                    ob, obb = ob16[it % 3]
                                    S.op("gpsimd", "tensor_tensor", out=ob[:], in0=t1[:], in1=gg[:], op=ALU.mult, R=[t1b, ggb], W=[obb])
                                    S.dma(C.ldq(), SC["hyT"][cc * 128:(cc + 1) * 128, tsl], ob[:], reads=[obb])


ALL_STAGES = ("inproj", "attn", "mlstm", "hyena", "merge", "ffn")


def build_program(n_layers=DEPTH, dbg=None, stages=ALL_STAGES, final=True):
    nc = bass.Bass("TRN2", target_bir_lowering=False)
    I = {}

    def inp(name, shape, dtype=F32):
        I[name] = nc.dram_tensor(name, list(shape), dtype, kind="ExternalInput").ap()

    inp("xT", [D, NT])
    inp("cvec", [128, KC, 2])
    inp("ada_w", [DEPTH, D, 6 * D])
    inp("ada_bT", [128, DEPTH, 48])
    inp("n1gT", [128, DEPTH, KC])
    inp("n2gT", [128, DEPTH, KC])
    inp("fgT", [128, KC])
    inp("w_in", [DEPTH, D, IN_W])
    inp("ropecos", [128, L])
    inp("ropesin", [128, L])
    inp("ident", [128, 128])
    inp("amask", [128, 2, 512], BF16)
    inp("tri", [128, 2, 128])
    inp("sinkR", [128, DEPTH * 8])
    inp("mlgbR", [128, DEPTH * 16])
    inp("mlngR", [128, DEPTH * 512])
    inp("w_branch", [DEPTH, 3, 512, D])
    inp("w_out", [DEPTH, D, D])
    inp("w_up", [DEPTH, D, 2 * DFF])
    inp("w_down", [DEPTH, DFF, D])
    inp("fcwT", [128, DEPTH, 3, 44])
    inp("fcbT", [128, DEPTH, 44])
    hyena_inputs(inp)

    SC = {}

    def scr(name, shape, dtype):
        kind = "ExternalOutput" if (dbg and name in dbg) else "Internal"
        SC[name] = nc.dram_tensor(name, list(shape), dtype, kind=kind).ap()

    scr("qT", [512, NT], BF16)
    scr("kT", [128, NT], BF16)
    scr("vtok", [NT, 128], BF16)
    scr("mlqT", [512, NT], BF16)
    scr("mlkT", [512, NT], BF16)
    scr("mlktok", [NT, 512], BF16)
    scr("mlvtok", [NT, 512], BF16)
    scr("mlotok", [NT, 512], F32)
    scr("mlgtok", [NT, 16], F32)
    scr("hyu", [1536, NT], F32)
    scr("bgT", [3072, NT], BF16)
    scr("attT", [512, NT], BF16)
    scr("mlsT", [512, NT], BF16)
    scr("hyT", [512, NT], BF16)
    scr("aT", [DFF, NT], BF16)
    scr("xs", [D, NT], F32)
    hyena_scratch(scr)
    outT = nc.dram_tensor("outT", [D, L], F32, kind="ExternalOutput").ap()

    with ExitStack() as es:
        S = Sched(nc, es)
        C = Ctx(nc, S)
        C.scrb = None
        C.outb = Buf("out")
        C.psum = []
        C.psb = []
        for i in range(8):
            C.psum.append(es.enter_context(nc.psum_tensor("ps%d" % i, [128, 512], F32)))
            C.psb.append(Buf("ps%d" % i))
        C.constb = Buf("const")
        C.ones_bf = es.enter_context(nc.sbuf_tensor("ones_bf", [128, 128], BF16))
        S.op("gpsimd", "memset", C.ones_bf[:], 1.0, W=[C.constb])
        C.negpi = es.enter_context(nc.sbuf_tensor("negpi", [128, 1], F32))
        S.op("gpsimd", "memset", C.negpi[:], -math.pi, W=[C.constb])
        C.ident = es.enter_context(nc.sbuf_tensor("ident_sb", [128, 128], F32))
        S.dma("sync", C.ident[:], I["ident"][:, :], writes=[C.constb])
        C.amask = es.enter_context(nc.sbuf_tensor("amask_sb", [128, 2, 512], BF16))
        S.dma("sync", C.amask[:], I["amask"][:, :, :], writes=[C.constb])
        C.tri = es.enter_context(nc.sbuf_tensor("tri_sb", [128, 2, 128], F32))
        S.dma("sync", C.tri[:], I["tri"][:, :, :], writes=[C.constb])
        mod = [es.enter_context(nc.sbuf_tensor("mod%d" % l, [128, 48, 2], F32)) for l in range(DEPTH)]
        mod_b = [Buf("mod%d" % l) for l in range(DEPTH)]
        n1g = es.enter_context(nc.sbuf_tensor("n1g", [128, DEPTH, KC], F32))
        n2g = es.enter_context(nc.sbuf_tensor("n2g", [128, DEPTH, KC], F32))
        fg = es.enter_context(nc.sbuf_tensor("fg", [128, KC], F32))
        ngb = Buf("ng")
        S.dma("sync", n1g[:], I["n1gT"][:, :, :], writes=[ngb])
        S.dma("sync", n2g[:], I["n2gT"][:, :, :], writes=[ngb])
        S.dma("sync", fg[:], I["fgT"][:, :], writes=[ngb])
        AB = es.enter_context(nc.sbuf_tensor("AB", [128, DEPTH, 2, KC, 2], F32))
        ABb = Buf("AB")

        stage_mod(C, I, mod, mod_b)
        for l in range(DEPTH):
            for which, ng, off in ((0, n1g, 8), (1, n2g, 32)):
                for w in range(2):
                    S.op("vector", "scalar_tensor_tensor", out=AB[:, l, which, :, w], in0=mod[l][:, off:off + 8, w], scalar=1.0,
                         in1=ng[:, l, :], op0=ALU.add, op1=ALU.mult, R=[mod_b[l], ngb], W=[ABb])
        S.barrier()

        for l in range(n_layers):
            need_ctx = l < DEPTH - 1
            xsrc = I["xT"] if l == 0 else SC["xs"]
            if "inproj" in stages:
                stage_inproj(C, I, l, xsrc, AB[:, l, 0, :, :], mod[l][:, 0:8, :], ABb, SC)
            if "attn" in stages:
                stage_attn(C, I, l, SC, need_ctx)
            if "mlstm" in stages:
                stage_mlstm(C, I, l, SC, need_ctx)
            if "hyena" in stages:
                stage_hyena(C, I, l, SC, need_ctx)
            if "merge" in stages:
                stage_merge(C, I, l, SC, xsrc, SC["xs"], mod[l][:, 16:24, :], mod_b[l], need_ctx)
            if "ffn" in stages:
                stage_ffn_up(C, I, l, SC, SC["xs"], AB[:, l, 1, :, :], mod[l][:, 24:32, :], ABb, need_ctx)
                stage_ffn_down(C, I, l, SC, SC["xs"], mod[l][:, 40:48, :], mod_b[l], need_ctx)
        if final:
            stage_final(C, I, SC["xs"], outT, fg, ngb)
        else:
            with Stage(C) as st:
                t, tb = st.tile([128, 64], F32, "o")
                S.op("vector", "tensor_copy", out=t[:], in_=mod[0][:, 0:32, :].rearrange("p a b -> p (a b)"), R=[mod_b[0]], W=[tb])
                S.dma("sync", outT[0:128, 0:64], t[:], reads=[tb], writes=[C.outb])
        S.barrier()
        S.emit()
    return nc, SC


def _rope_tables():
    rows = L // 64
    row = np.repeat(np.arange(rows, dtype=np.float32), 64)
    col = np.tile(np.arange(64, dtype=np.float32), rows)
    nf = 16
    inv = (np.float32(10000.0) ** (-np.arange(nf, dtype=np.float32) / nf)).astype(np.float32)
    ang = np.concatenate([row[:, None] * inv, col[:, None] * inv], axis=-1).astype(np.float32)
    c = np.cos(ang).T.astype(np.float32)
    s = np.sin(ang).T.astype(np.float32)
    c64 = np.concatenate([c, c], 0)
    s64 = np.concatenate([-s, s], 0)
    return np.ascontiguousarray(np.concatenate([c64, c64], 0)), np.ascontiguousarray(np.concatenate([s64, s64], 0))


def _pk(v):
    v = np.asarray(v, np.float32)
    lead = v.shape[:-1]
    k = v.shape[-1] // 128
    return np.ascontiguousarray(np.moveaxis(v.reshape(*lead, k, 128), -1, 0))


def _rep(v):
    v = np.asarray(v, np.float32).reshape(1, -1)
    return np.ascontiguousarray(np.repeat(v, 128, axis=0))


_CONST = {}


def _consts():
    if _CONST:
        return _CONST
    _CONST["rope"] = _rope_tables()
    i = np.arange(128)
    mp = (i[:, None] >= i[None, :]).astype(np.float32)
    mn = (i[:, None] <= i[None, :]).astype(np.float32)
    am = np.stack([np.tile(mp, (1, 4)), np.tile(mn, (1, 4))], axis=1)
    _CONST["amask"] = np.ascontiguousarray(am).astype(ml_dtypes.bfloat16)
    _CONST["tri"] = np.ascontiguousarray(np.stack([mn, mp], axis=1)).astype(np.float32)
    _CONST["ident"] = np.eye(128, dtype=np.float32)
    return _CONST


def prep_shared_inputs(inputs):
    cst = _consts()
    m = {}
    for k in ("ada_w", "w_in", "w_branch", "w_out", "w_up", "w_down"):
        m[k] = np.ascontiguousarray(inputs[k], dtype=np.float32)
    m["ada_bT"] = _pk(inputs["ada_b"])
    m["n1gT"] = _pk(inputs["norm1_g"])
    m["n2gT"] = _pk(inputs["norm2_g"])
    m["fgT"] = _pk(inputs["final_g"])
    m["ropecos"], m["ropesin"] = cst["rope"]
    m["ident"] = cst["ident"]
    m["amask"] = cst["amask"]
    m["tri"] = cst["tri"]
    m["sinkR"] = _rep(inputs["att_sink"])
    m["mlgbR"] = _rep(inputs["ml_gate_b"])
    m["mlngR"] = _rep(inputs["ml_norm_g"])
    m["fcwT"] = _pk(inputs["ffn_conv_w"])
    m["fcbT"] = _pk(inputs["ffn_conv_b"])
    hyena_host(inputs, m)
    return m


def prep_core_inputs(inputs, b, shared=None):
    m = dict(shared if shared is not None else prep_shared_inputs(inputs))
    xcat = np.concatenate([inputs["ctx"][b], inputs["x"][b]], axis=0)
    m["xT"] = np.ascontiguousarray(xcat.T, dtype=np.float32)
    cv = np.stack([inputs["c"][b], inputs["c_ctx"]], axis=-1)
    m["cvec"] = np.ascontiguousarray(cv.reshape(KC, 128, 2).transpose(1, 0, 2), dtype=np.float32)
    return m


_PROG = {}


def kernel(**inputs):
    inputs = {k: np.asarray(v) for k, v in inputs.items()}
    if "nc" not in _PROG:
        _PROG["nc"] = build_program()[0]
    nc = _PROG["nc"]
    shared = prep_shared_inputs(inputs)
    B = inputs["x"].shape[0]
    in_maps = [prep_core_inputs(inputs, c % B, shared) for c in range(8)]
    res = run_bass_kernel_spmd(nc, in_maps, core_ids=list(range(8)))
    out = np.stack([np.asarray(res.results[b]["outT"], dtype=np.float32).T for b in range(B)], axis=0)
    return np.ascontiguousarray(out)
```

```python
import math
from contextlib import ExitStack
import numpy as np
import ml_dtypes
import concourse.bass as bass
import concourse.mybir as mybir
from concourse.bass_utils import run_bass_kernel_spmd

F32 = mybir.dt.float32
BF16 = mybir.dt.bfloat16
ALU = mybir.AluOpType
ACT = mybir.ActivationFunctionType
AX = mybir.AxisListType

SAME_ENGINE_SYNC = True

D = 1024
KC = 8
L = 4096
LC = 256
NT = L + LC
NTILE = NT // 128
DEPTH = 4
EPS = 1e-6
IN_W = 7440
DFF = 2816
O_Q, O_K, O_V, O_MQ, O_MK, O_MV, O_MO, O_MG, O_HY, O_BG = 0, 512, 640, 768, 1280, 1792, 2304, 2816, 2832, 4368
BLOCKS = [(0, 256)] + [(256 + 512 * i, 512) for i in range(8)]


class Buf:
    __slots__ = ("name", "last_w", "readers")

    def __init__(self, name=""):
        self.name = name
        self.last_w = None
        self.readers = {}


class Sched:
    COMPUTE = ("tensor", "vector", "scalar", "gpsimd")
    QUEUES = {"sync": 24, "gpsimd": 12, "scalar": 8}

    def __init__(self, nc, es):
        self.nc = nc
        self.streams = {e: [] for e in ("tensor", "vector", "scalar", "gpsimd", "sync")}
        self.sems = {}
        self.cnt = {}
        for e in self.COMPUTE:
            self.sems[("E", e)] = es.enter_context(nc.semaphore("s_" + e))
            self.cnt[("E", e)] = 0
        self.dq = {}
        for q, n in self.QUEUES.items():
            keys = []
            for i in range(n):
                k = ("D", q, i)
                self.sems[k] = es.enter_context(nc.semaphore("d_%s%d" % (q, i)))
                self.cnt[k] = 0
                keys.append(k)
            self.dq[q] = [keys, 0]
        self.waited = {}
        self.nops = 0

    def _deps(self, eng, reads, writes, is_dma=False):
        deps = {}

        def add(k, v, e):
            if k in deps:
                if deps[k][0] < v:
                    deps[k] = (v, e)
            else:
                deps[k] = (v, e)
        for b in reads:
            if b.last_w is not None:
                add(*b.last_w)
        for b in writes:
            if b.last_w is not None:
                add(*b.last_w)
            for k, (v, e) in b.readers.items():
                add(k, v, e)
        waits = []
        for k, (v, e) in deps.items():
            if k[0] == "E" and e == eng:
                if not is_dma and (eng == "tensor" or not SAME_ENGINE_SYNC):
                    continue
            wk = (eng, k)
            if self.waited.get(wk, -1) >= v:
                continue
            self.waited[wk] = v
            waits.append((k, v))
        return waits

    def _commit(self, tok, reads, writes):
        k, v, e = tok
        for b in reads:
            b.readers[k] = (v, e)
        for b in writes:
            b.last_w = tok
            b.readers = {}

    def op(self, eng, name, *args, R=(), W=(), **kw):
        reads, writes = R, W

        def fn(e, name=name, args=args, kw=kw):
            return getattr(e, name)(*args, **kw)
        waits = self._deps(eng, reads, writes)
        k = ("E", eng)
        self.cnt[k] += 1
        tok = (k, self.cnt[k], eng)
        self.streams[eng].append((waits, fn, k, 1))
        self._commit(tok, reads, writes)
        self.nops += 1
        return tok

    def dma(self, q, out, in_, reads=(), writes=(), **kw):
        reads = [b for b in reads if b is not None]
        writes = [b for b in writes if b is not None]
        keys, idx = self.dq[q]
        k = keys[idx % len(keys)]
        self.dq[q][1] = idx + 1
        waits = self._deps(q, reads, writes, is_dma=True)
        prev = self.cnt[k]
        if prev > 0:
            wk = (q, k)
            if self.waited.get(wk, -1) < prev:
                self.waited[wk] = prev
                waits.append((k, prev))
        self.cnt[k] += 16
        tok = (k, self.cnt[k], q)

        def fn(e, out=out, in_=in_, kw=kw):
            return e.dma_start(out=out, in_=in_, **kw)
        self.streams[q].append((waits, fn, k, 16))
        self._commit(tok, reads, writes)
        self.nops += 1
        return tok

    def barrier(self):
        for eng in self.streams:
            waits = []
            for k, v in self.cnt.items():
                if v == 0:
                    continue
                if k[0] == "E" and k[1] == eng:
                    continue
                wk = (eng, k)
                if self.waited.get(wk, -1) >= v:
                    continue
                self.waited[wk] = v
                waits.append((k, v))
            if waits:
                self.streams[eng].append((waits, None, None, 0))

    def emit(self):
        nc = self.nc
        sems = self.sems

        def replay(e, name):
            for waits, fn, k, inc in self.streams[name]:
                for (wk, v) in waits:
                    e.wait_ge(sems[wk], v)
                if fn is not None:
                    ins = fn(e)
                    ins.then_inc(sems[k], inc)
        with nc.Block() as block:
            @block.sync
            def _(e):
                replay(e, "sync")

            @block.tensor
            def _(e):
                replay(e, "tensor")

            @block.vector
            def _(e):
                replay(e, "vector")

            @block.scalar
            def _(e):
                replay(e, "scalar")

            @block.gpsimd
            def _(e):
                replay(e, "gpsimd")


class Ctx:
    def __init__(self, nc, S):
        self.nc = nc
        self.S = S
        self.psi = 0
        self.dq_i = 0
        self.reserved = set()

    def ps(self):
        while True:
            i = self.psi % 8
            self.psi += 1
            if i not in self.reserved:
                return self.psum[i], self.psb[i]

    def ldq(self):
        self.dq_i += 1
        return "sync" if self.dq_i % 3 else "gpsimd"


class Stage:
    uid = 0

    def __init__(self, C):
        self.C = C
        self.es = ExitStack()
        self.n = 0

    def __enter__(self):
        self.es.__enter__()
        return self

    def __exit__(self, *a):
        self.C.S.barrier()
        return self.es.__exit__(*a)

    def tile(self, shape, dtype, name=None):
        self.n += 1
        Stage.uid += 1
        nm = "%s_%d" % (name or "t", Stage.uid)
        t = self.es.enter_context(self.C.nc.sbuf_tensor(nm, list(shape), dtype))
        return t, Buf(nm)


def stage_mod(C, I, mod, mod_b):
    S = C.S
    with Stage(C) as st:
        cv, cvb = st.tile([128, KC, 2], F32, "cv")
        S.dma("sync", cv[:], I["cvec"][:, :, :], writes=[cvb])
        scv, scvb = st.tile([128, KC, 2], F32, "scv")
        S.op("scalar", "activation", out=scv[:], in_=cv[:], func=ACT.Silu, R=[cvb], W=[scvb])
        wbufs = [st.tile([128, KC, 1024], F32, "adaw") for _ in range(2)]
        adab, adabb = st.tile([128, DEPTH, 48], F32, "adab")
        S.dma("sync", adab[:], I["ada_bT"][:, :, :], writes=[adabb])
        it = 0
        for l in range(DEPTH):
            wv = I["ada_w"][l].rearrange("(k p) n -> p k n", p=128)
            pst, psb = C.ps()
            for slab in range(6):
                wt, wtb = wbufs[it % 2]
                it += 1
                for k in range(KC):
                    S.dma("sync" if k % 2 else "gpsimd", wt[:, k, :], wv[:, k, slab * 1024:(slab + 1) * 1024], writes=[wtb])
                for jj in range(8):
                    j = slab * 8 + jj
                    for k in range(KC):
                        S.op("tensor", "matmul", pst[:, 2 * j:2 * j + 2], lhsT=wt[:, k, jj * 128:(jj + 1) * 128], rhs=scv[:, k, :],
                             start=(k == 0), stop=(k == KC - 1), R=[wtb, scvb], W=[psb])
            for w in range(2):
                S.op("vector", "tensor_tensor", out=mod[l][:, :, w],
                     in0=pst[:, 0:96].rearrange("p (j w) -> p j w", w=2)[:, :, w], in1=adab[:, l, :], op=ALU.add,
                     R=[psb, adabb], W=[mod_b[l]])


def load_weight_bf16(C, wsrc, ncols, pool, kc=KC):
    S = C.S
    i = pool["i"]
    pool["i"] += 1
    wf, wfb = pool["f"][i % len(pool["f"])]
    wb, wbb = pool["b"][i % len(pool["b"])]
    half = kc // 2
    S.dma("sync", wf[:, 0:half, 0:ncols], wsrc[:, 0:half, :], writes=[wfb])
    S.dma("gpsimd", wf[:, half:kc, 0:ncols], wsrc[:, half:kc, :], writes=[wfb])
    S.op("gpsimd", "tensor_copy", out=wb[:, 0:half, 0:ncols], in_=wf[:, 0:half, 0:ncols], R=[wfb], W=[wbb])
    S.op("vector", "tensor_copy", out=wb[:, half:kc, 0:ncols], in_=wf[:, half:kc, 0:ncols], R=[wfb], W=[wbb])
    return wb, wbb


def make_wpool(st, kc=KC, ncols=512, nbuf=2):
    return {"i": 0, "f": [st.tile([128, kc, ncols], F32, "wf") for _ in range(nbuf)],
            "b": [st.tile([128, kc, ncols], BF16, "wb") for _ in range(nbuf)]}


def rms_rstd(C, st, xt, xtb, n, rs, rsb, sqs):
    S = C.S
    pst, psb = C.ps()
    for k in range(KC):
        sq, sqb = sqs[k % 2]
        S.op("scalar", "activation", out=sq[:, 0:n], in_=xt[:, k, 0:n], func=ACT.Square, R=[xtb], W=[sqb])
        S.op("tensor", "matmul", pst[:, 0:n], lhsT=C.ones_bf[:], rhs=sq[:, 0:n], start=(k == 0), stop=(k == KC - 1),
             R=[sqb, C.constb], W=[psb])
    S.op("scalar", "activation", out=rs[:, 0:n], in_=pst[:, 0:n], func=ACT.Ln, bias=EPS, scale=1.0 / D, R=[psb], W=[rsb])
    S.op("scalar", "activation", out=rs[:, 0:n], in_=rs[:, 0:n], func=ACT.Exp, scale=-0.5, R=[rsb], W=[rsb])


def norm_mod(C, st, xsrc, hT, hTb, A, B, Ab, blocks):
    S = C.S
    xb = [st.tile([128, KC, 512], F32, "xblk") for _ in range(2)]
    sqs = [st.tile([128, 512], BF16, "sq") for _ in range(2)]
    rs, rsb = st.tile([128, 512], F32, "rstd")
    tm = [st.tile([128, 512], F32, "tmpn") for _ in range(2)]
    xv = xsrc.rearrange("(k p) n -> p k n", p=128)
    for bi, (t0, n) in blocks:
        w = 1 if bi == 0 else 0
        xt, xtb = xb[bi % 2]
        S.dma("sync", xt[:, 0:4, 0:n], xv[:, 0:4, t0:t0 + n], reads=[C.scrb], writes=[xtb])
        S.dma("gpsimd", xt[:, 4:8, 0:n], xv[:, 4:8, t0:t0 + n], reads=[C.scrb], writes=[xtb])
        rms_rstd(C, st, xt, xtb, n, rs, rsb, sqs)
        for k in range(KC):
            t, tb = tm[k % 2]
            S.op("vector", "scalar_tensor_tensor", out=t[:, 0:n], in0=xt[:, k, 0:n], scalar=A[:, k, w:w + 1], in1=rs[:, 0:n],
                 op0=ALU.mult, op1=ALU.mult, R=[xtb, rsb, Ab], W=[tb])
            S.op("scalar", "activation", out=hT[:, k, t0:t0 + n], in_=t[:, 0:n], func=ACT.Identity, bias=B[:, k, w:w + 1], scale=1.0,
                 R=[tb, Ab], W=[hTb])


def proj_fm(C, hT, hTb, wb, wbb, col0, M, blocks, epilogue, kc=KC):
    S = C.S
    for bi, (t0, n) in blocks:
        pst, psb = C.ps()
        for k in range(kc):
            S.op("tensor", "matmul", pst[0:M, 0:n], lhsT=wb[:, k, col0:col0 + M], rhs=hT[:, k, t0:t0 + n],
                 start=(k == 0), stop=(k == kc - 1), R=[wbb, hTb], W=[psb])
        epilogue(bi, t0, n, pst, psb)


def proj_tm(C, hT, hTb, wb, wbb, col0, ncols, epilogue, kc=KC, tiles=None):
    S = C.S
    for ti in (tiles if tiles is not None else range(NTILE)):
        pst, psb = C.ps()
        for k in range(kc):
            S.op("tensor", "matmul", pst[:, 0:ncols], lhsT=hT[:, k, ti * 128:(ti + 1) * 128], rhs=wb[:, k, col0:col0 + ncols],
                 start=(k == 0), stop=(k == kc - 1), R=[wbb, hTb], W=[psb])
        epilogue(ti, pst, psb)


class Stager:
    def __init__(self, st, shape, dtype, n=3, name="stg"):
        self.bufs = [st.tile(shape, dtype, name) for _ in range(n)]
        self.i = 0

    def get(self):
        t = self.bufs[self.i % len(self.bufs)]
        self.i += 1
        return t


def stage_inproj(C, I, l, xsrc, A, B, Ab, SC):
    S = C.S
    ALLB = list(enumerate(BLOCKS))
    with Stage(C) as st:
        hT, hTb = st.tile([128, KC, NT], BF16, "hT")
        norm_mod(C, st, xsrc, hT, hTb, A, B, Ab, ALLB)
        wpool = make_wpool(st)
        wv = I["w_in"][l].rearrange("(k p) n -> p k n", p=128)
        stg_b = Stager(st, [128, 512], BF16, 4, "stgb")
        stg_f = Stager(st, [128, 512], F32, 3, "stgf")
        ev = [0]

        def evac_copy(dst_dram, stager, scale=None, func=None):
            def ep(bi, t0, n, pst, psb):
                sb, sbb = stager.get()
                ev[0] += 1
                if func is not None:
                    S.op("scalar", "activation", out=sb[:, 0:n], in_=pst[:, 0:n], func=func, R=[psb], W=[sbb])
                elif scale is not None:
                    S.op("scalar", "mul", out=sb[:, 0:n], in_=pst[:, 0:n], mul=scale, R=[psb], W=[sbb])
                elif ev[0] % 2:
                    S.op("scalar", "copy", out=sb[:, 0:n], in_=pst[:, 0:n], R=[psb], W=[sbb])
                else:
                    S.op("vector", "tensor_copy", out=sb[:, 0:n], in_=pst[:, 0:n], R=[psb], W=[sbb])
                S.dma(C.ldq(), dst_dram[:, t0:t0 + n], sb[:, 0:n], reads=[sbb], writes=[C.scrb])
            return ep

        rope_c = [st.tile([128, 512], F32, "ropec") for _ in range(2)]
        rope_s = [st.tile([128, 512], F32, "ropes") for _ in range(2)]
        rtmp = [st.tile([128, 512], F32, "rtmp") for _ in range(2)]
        rtmp2 = [st.tile([128, 512], F32, "rtmp2") for _ in range(2)]
        for (c0, nchunk, dst) in ((O_Q, 4, SC["qT"]), (O_K, 1, SC["kT"])):
            W = nchunk * 128
            wb, wbb = load_weight_bf16(C, wv[:, :, c0:c0 + W], W, wpool)
            i = wpool["i"]
            wpool["i"] += 1
            wf2, wf2b = wpool["f"][i % 2]
            wb2, wb2b = wpool["b"][i % 2]
            src = wv[:, :, c0:c0 + W].rearrange("p k (h two i) -> p k h two i", two=2, i=32)
            dstv = wf2[:, :, 0:W].rearrange("p k (h two i) -> p k h two i", two=2, i=32)
            for k in range(KC):
                S.dma("sync", dstv[:, k, :, 1, :], src[:, k, :, 0, :], writes=[wf2b])
                S.dma("gpsimd", dstv[:, k, :, 0, :], src[:, k, :, 1, :], writes=[wf2b])
            S.op("gpsimd", "tensor_copy", out=wb2[:, :, 0:W], in_=wf2[:, :, 0:W], R=[wf2b], W=[wb2b])
            for bi, (t0, n) in ALLB:
                rc = rsn = rcb = rsnb = None
                if bi > 0:
                    rc, rcb = rope_c[bi % 2]
                    rsn, rsnb = rope_s[bi % 2]
                    S.dma("sync", rc[:, 0:n], I["ropecos"][:, t0 - LC:t0 - LC + n], writes=[rcb])
                    S.dma("sync", rsn[:, 0:n], I["ropesin"][:, t0 - LC:t0 - LC + n], writes=[rsnb])
                for ch in range(nchunk):
                    pst, psb = C.ps()
                    for k in range(KC):
                        S.op("tensor", "matmul", pst[:, 0:n], lhsT=wb[:, k, ch * 128:(ch + 1) * 128], rhs=hT[:, k, t0:t0 + n],
                             start=(k == 0), stop=(k == KC - 1), R=[wbb, hTb], W=[psb])
                    sb, sbb = stg_b.get()
                    if bi == 0:
                        S.op("scalar", "copy", out=sb[:, 0:n], in_=pst[:, 0:n], R=[psb], W=[sbb])
                    else:
                        pst2, psb2 = C.ps()
                        for k in range(KC):
                            S.op("tensor", "matmul", pst2[:, 0:n], lhsT=wb2[:, k, ch * 128:(ch + 1) * 128], rhs=hT[:, k, t0:t0 + n],
                                 start=(k == 0), stop=(k == KC - 1), R=[wb2b, hTb], W=[psb2])
                        t1, t1b = rtmp[ch % 2]
                        t2, t2b = rtmp2[ch % 2]
                        S.op("vector", "tensor_tensor", out=t1[:, 0:n], in0=pst[:, 0:n], in1=rc[:, 0:n], op=ALU.mult, R=[psb, rcb], W=[t1b])
                        S.op("vector", "tensor_tensor", out=t2[:, 0:n], in0=pst2[:, 0:n], in1=rsn[:, 0:n], op=ALU.mult, R=[psb2, rsnb], W=[t2b])
                        S.op("gpsimd", "tensor_tensor", out=sb[:, 0:n], in0=t1[:, 0:n], in1=t2[:, 0:n], op=ALU.add, R=[t1b, t2b], W=[sbb])
                    S.dma(C.ldq(), dst[ch * 128:(ch + 1) * 128, t0:t0 + n], sb[:, 0:n], reads=[sbb], writes=[C.scrb])

        def fm_group(c0, nchunk, dst, stager, scale=None, func=None):
            done = 0
            while done < nchunk:
                g = min(4, nchunk - done)
                wb, wbb = load_weight_bf16(C, wv[:, :, c0 + done * 128:c0 + (done + g) * 128], g * 128, wpool)
                for ch in range(g):
                    row0 = (done + ch) * 128
                    proj_fm(C, hT, hTb, wb, wbb, ch * 128, 128, ALLB, evac_copy(dst[row0:row0 + 128, :], stager, scale=scale, func=func))
                done += g

        fm_group(O_MQ, 4, SC["mlqT"], stg_b)
        fm_group(O_MK, 4, SC["mlkT"], stg_b, scale=128.0 ** -0.5)
        fm_group(O_HY, 12, SC["hyu"], stg_f)
        fm_group(O_BG, 24, SC["bgT"], stg_b, func=ACT.Sigmoid)

        def tm_group(c0, ncols, dst, stager, scale=None):
            wb, wbb = load_weight_bf16(C, wv[:, :, c0:c0 + ncols], ncols, wpool)

            def ep(ti, pst, psb):
                sb, sbb = stager.get()
                if scale is not None:
                    S.op("scalar", "mul", out=sb[:, 0:ncols], in_=pst[:, 0:ncols], mul=scale, R=[psb], W=[sbb])
                elif ti % 2:
                    S.op("scalar", "copy", out=sb[:, 0:ncols], in_=pst[:, 0:ncols], R=[psb], W=[sbb])
                else:
                    S.op("vector", "tensor_copy", out=sb[:, 0:ncols], in_=pst[:, 0:ncols], R=[psb], W=[sbb])
                S.dma(C.ldq(), dst[ti * 128:(ti + 1) * 128, 0:ncols], sb[:, 0:ncols], reads=[sbb], writes=[C.scrb])
            proj_tm(C, hT, hTb, wb, wbb, 0, ncols, ep)

        tm_group(O_V, 128, SC["vtok"], stg_b)
        tm_group(O_MK, 512, SC["mlktok"], stg_b, scale=128.0 ** -0.5)
        tm_group(O_MV, 512, SC["mlvtok"], stg_b)
        tm_group(O_MO, 512, SC["mlotok"], stg_f)
        tm_group(O_MG, 16, SC["mlgtok"], stg_f)


def stage_attn(C, I, l, SC, need_ctx):
    S = C.S
    with Stage(C) as st:
        kt, ktb = st.tile([64, 2, NT], BF16, "kt")
        va, vab = st.tile([128, NTILE, 2, 65], BF16, "vaug")
        S.op("gpsimd", "memset", va[:], 1.0, W=[vab])
        vsrc = SC["vtok"].rearrange("(t p) (g d) -> p t g d", p=128, g=2)
        qg = [st.tile([64, 4, NT], BF16, "qg") for _ in range(2)]
        for g in range(2):
            S.dma("sync", kt[:, g, :], SC["kT"][g * 64:(g + 1) * 64, :], reads=[C.scrb], writes=[ktb])
            S.dma("gpsimd", va[:, :, g, 0:64], vsrc[:, :, g, :], reads=[C.scrb], writes=[vab])
            for j in range(4):
                h = 4 * g + j
                S.dma("sync" if j % 2 else "gpsimd", qg[g][0][:, j, :], SC["qT"][h * 64:(h + 1) * 64, :], reads=[C.scrb], writes=[qg[g][1]])
        es, esb = st.tile([128, 8], F32, "esink")
        S.dma("sync", es[:], I["sinkR"][:, l * 8:(l + 1) * 8], writes=[esb])
        S.op("scalar", "activation", out=es[:], in_=es[:], func=ACT.Exp, R=[esb], W=[esb])
        ebufs = [st.tile([128, 512], BF16, "E") for _ in range(6)]
        ei = [0]
        osb = [st.tile([128, 256], F32, "osb") for _ in range(2)]
        den = [st.tile([128, 4], F32, "den") for _ in range(2)]
        stg = Stager(st, [128, 2, 128], BF16, 3, "astg")
        it = 0
        qtiles = ([0, 1] if need_ctx else []) + list(range(2, NTILE))
        for ti in qtiles:
            if ti < 2:
                keys = [(0, None), (1, None)]
            else:
                keys = [(0, None), (1, None)]
                if ti > 2:
                    keys.append((ti - 1, 0))
                keys.append((ti, None))
                if ti < NTILE - 1:
                    keys.append((ti + 1, 1))
            for g in range(2):
                it += 1
                qt, qtb = qg[g]
                es_l = []
                for idx, (kti, mk) in enumerate(keys):
                    pst, psb = C.ps()
                    S.op("tensor", "matmul", pst[:, :].rearrange("p (j t) -> p j t", j=4), lhsT=kt[:, g, kti * 128:(kti + 1) * 128],
                         rhs=qt[:, :, ti * 128:(ti + 1) * 128], start=True, stop=True, R=[ktb, qtb], W=[psb])
                    e, eb = ebufs[ei[0] % 6]
                    ei[0] += 1
                    S.op("scalar", "activation", out=e[:], in_=pst[:, :], func=ACT.Exp, scale=0.125, R=[psb], W=[eb])
                    if mk is not None:
                        S.op("gpsimd" if mk else "vector", "tensor_tensor", out=e[:], in0=e[:], in1=C.amask[:, mk, :], op=ALU.mult,
                             R=[eb, C.constb], W=[eb])
                    es_l.append((e, eb, kti))
                pso, psob = C.ps()
                for j in range(4):
                    for idx, (e, eb, kti) in enumerate(es_l):
                        S.op("tensor", "matmul", pso[:, j * 65:(j + 1) * 65], lhsT=e[:, j * 128:(j + 1) * 128], rhs=va[:, kti, g, :],
                             start=(idx == 0), stop=(idx == len(es_l) - 1), R=[eb, vab], W=[psob])
                dn, dnb = den[it % 2]
                S.op("vector", "tensor_tensor", out=dn[:], in0=pso[:, 0:260].rearrange("p (j c) -> p j c", c=65)[:, :, 64],
                     in1=es[:, 4 * g:4 * g + 4], op=ALU.add, R=[psob, esb], W=[dnb])
                S.op("vector", "reciprocal", out=dn[:], in_=dn[:], R=[dnb], W=[dnb])
                o, ob = osb[it % 2]
                for j in range(4):
                    if j % 2:
                        S.op("vector", "tensor_scalar", out=o[:, j * 64:(j + 1) * 64], in0=pso[:, j * 65:j * 65 + 64], scalar1=dn[:, j:j + 1],
                             scalar2=None, op0=ALU.mult, R=[psob, dnb], W=[ob])
                    else:
                        S.op("scalar", "activation", out=o[:, j * 64:(j + 1) * 64], in_=pso[:, j * 65:j * 65 + 64], func=ACT.Copy,
                             scale=dn[:, j:j + 1], R=[psob, dnb], W=[ob])
                sg, sgb = stg.get()
                for c in range(2):
                    pt, ptb = C.ps()
                    S.op("tensor", "transpose", pt[:, 0:128], o[:, c * 128:(c + 1) * 128], C.ident[:], R=[ob, C.constb], W=[ptb])
                    if c:
                        S.op("vector", "tensor_copy", out=sg[:, c, :], in_=pt[:, 0:128], R=[ptb], W=[sgb])
                    else:
                        S.op("scalar", "copy", out=sg[:, c, :], in_=pt[:, 0:128], R=[ptb], W=[sgb])
                S.dma(C.ldq(), SC["attT"][g * 256:(g + 1) * 256, ti * 128:(ti + 1) * 128].rearrange("(c p) t -> p c t", p=128), sg[:],
                      reads=[sgb], writes=[C.scrb])
def stage_mlstm(C, I, l, SC, need_ctx):
    S = C.S
    with Stage(C) as st:
        gt, gtb = st.tile([128, NTILE, 16], F32, "gt")
        S.dma("sync", gt[:], SC["mlgtok"].rearrange("(t p) c -> p t c", p=128), reads=[C.scrb], writes=[gtb])
        gb, gbb = st.tile([128, 16], F32, "gbias")
        S.dma("sync", gb[:], I["mlgbR"][:, l * 16:(l + 1) * 16], writes=[gbb])
        for t in range(NTILE):
            S.op("vector" if t % 2 else "gpsimd", "tensor_tensor", out=gt[:, t, :], in0=gt[:, t, :], in1=gb[:], op=ALU.add, R=[gtb, gbb], W=[gtb])
        nl, nlb = st.tile([128, NTILE, 16], F32, "nl")
        S.op("scalar", "activation", out=nl[:], in_=gt[:], func=ACT.Exp, scale=-1.0, R=[gtb], W=[nlb])
        S.op("scalar", "activation", out=nl[:], in_=nl[:], func=ACT.Ln, bias=1.0, scale=1.0, R=[nlb], W=[nlb])
        onesf, onesfb = st.tile([128, 128], F32, "onesf")
        S.op("gpsimd", "memset", onesf[:], 1.0, W=[onesfb])
        w_, eb_, ebl_ = [], [], []
        for dd in range(2):
            fc = 4 + 8 * dd
            ic = 8 * dd
            nlc, nlcb = st.tile([128, NTILE, 4], F32, "nlc")
            S.op("vector", "tensor_copy", out=nlc[:], in_=nl[:, :, fc:fc + 4], R=[nlb], W=[nlcb])
            p1, p1b = C.ps()
            S.op("tensor", "matmul", p1[:, 0:NTILE * 4], lhsT=C.tri[:, dd, :], rhs=nlc[:].rearrange("p t c -> p (t c)"), start=True, stop=True,
                 R=[nlcb, C.constb], W=[p1b])
            p2, p2b = C.ps()
            S.op("tensor", "matmul", p2[:, 0:NTILE * 4], lhsT=onesf[:], rhs=nlc[:].rearrange("p t c -> p (t c)"), start=True, stop=True,
                 R=[nlcb, onesfb], W=[p2b])
            w, wb_ = st.tile([128, NTILE, 4], F32, "w")
            eb, ebb = st.tile([128, NTILE, 4], F32, "eb")
            ebl, eblb = st.tile([128, NTILE, 4], F32, "ebl")
            S.op("vector", "tensor_tensor", out=w[:], in0=p1[:, 0:NTILE * 4].rearrange("p (t c) -> p t c", c=4), in1=gt[:, :, ic:ic + 4],
                 op=ALU.add, R=[p1b, gtb], W=[wb_])
            S.op("scalar", "activation", out=w[:], in_=w[:], func=ACT.Exp, R=[wb_], W=[wb_])
            S.op("scalar", "activation", out=eb[:].rearrange("p t c -> p (t c)"), in_=p1[:, 0:NTILE * 4], func=ACT.Exp, scale=-1.0, R=[p1b], W=[ebb])
            S.op("scalar", "activation", out=ebl[:].rearrange("p t c -> p (t c)"), in_=p2[:, 0:NTILE * 4], func=ACT.Exp, scale=-1.0, R=[p2b], W=[eblb])
            w_.append((w, wb_))
            eb_.append((eb, ebb))
            ebl_.append((ebl, eblb))
        gain, gainb = st.tile([128, 512], F32, "gain")
        S.dma("sync", gain[:], I["mlngR"][:, l * 512:(l + 1) * 512], writes=[gainb])

        hb = []
        for i in range(2):
            hb.append(dict(q=st.tile([128, NT], BF16, "mq"), k=st.tile([128, NT], BF16, "mk"),
                           kt=st.tile([128, NTILE, 128], BF16, "mkt"), v=st.tile([128, NTILE, 129], BF16, "mv"),
                           o=st.tile([128, NTILE, 128], F32, "mo"),
                           h=[st.tile([128, NTILE, 128], F32, "hf"), st.tile([128, NTILE, 128], F32, "hbk")]))
        NR = 8
        Abuf = [st.tile([128, 128], BF16, "A") for _ in range(NR)]
        Vw = [st.tile([128, 129], BF16, "Vw") for _ in range(NR)]
        dnb_ = [st.tile([128, 2], F32, "dn") for _ in range(NR)]
        hs_ = [st.tile([128, 128], F32, "hs") for _ in range(3)]
        sq_ = [st.tile([128, 128], F32, "hsq") for _ in range(3)]
        sg_ = [st.tile([128, 128], F32, "hsg") for _ in range(3)]
        ss_ = [st.tile([128, 2], F32, "ss") for _ in range(3)]
        y_ = [st.tile([128, 128], F32, "y") for _ in range(3)]
        stg = Stager(st, [128, 128], BF16, 3, "mstg")
        states = [[(st.tile([128, 129], F32, "Caug"), st.tile([128, 129], F32, "Ctmp"), st.tile([128, 129], BF16, "Cbf"))
                   for _ in range(2)] for _ in range(2)]
        orders = [list(range(NTILE)), [1, 0] + list(range(NTILE - 1, 1, -1))]
        it = 0
        for pair in range(2):
            for hi in range(2):
                j = 2 * pair + hi
                H = hb[hi]
                q, qb = H["q"]
                k, kb = H["k"]
                ktk, ktkb = H["kt"]
                v, vb = H["v"]
                o, ob = H["o"]
                S.dma("sync", q[:], SC["mlqT"][j * 128:(j + 1) * 128, :], writes=[qb])
                S.dma("gpsimd", k[:], SC["mlkT"][j * 128:(j + 1) * 128, :], writes=[kb])
                S.dma("sync", ktk[:], SC["mlktok"].rearrange("(t p) c -> p t c", p=128)[:, :, j * 128:(j + 1) * 128], writes=[ktkb])
                S.op("gpsimd", "memset", v[:], 1.0, W=[vb])
                S.dma("gpsimd", v[:, :, 0:128], SC["mlvtok"].rearrange("(t p) c -> p t c", p=128)[:, :, j * 128:(j + 1) * 128], writes=[vb])
                S.dma("sync", o[:], SC["mlotok"].rearrange("(t p) c -> p t c", p=128)[:, :, j * 128:(j + 1) * 128], writes=[ob])
                for dd in range(2):
                    (Ca, Cab), (Ct, Ctb), (Cb, Cbb) = states[hi][dd]
                    S.op("gpsimd", "memset", Ca[:], 0.0, W=[Cab])
                    S.op("gpsimd", "memset", Cb[:], 0.0, W=[Cbb])
            for step in range(NTILE):
                for hi in range(2):
                    j = 2 * pair + hi
                    H = hb[hi]
                    q, qb = H["q"]
                    k, kb = H["k"]
                    ktk, ktkb = H["kt"]
                    v, vb = H["v"]
                    for dd in range(2):
                        w, wb_ = w_[dd]
                        eb, ebb = eb_[dd]
                        ebl, eblb = ebl_[dd]
                        (Ca, Cab), (Ct, Ctb), (Cb, Cbb) = states[hi][dd]
                        hh, hhb = H["h"][dd]
                        c = orders[dd][step]
                        it += 1
                        cs = slice(c * 128, (c + 1) * 128)
                        pS, pSb = C.ps()
                        S.op("tensor", "matmul", pS[:, 0:128], lhsT=k[:, cs], rhs=q[:, cs], start=True, stop=True, R=[kb, qb], W=[pSb])
                        A, Ab_ = Abuf[it % NR]
                        S.op("vector", "scalar_tensor_tensor", out=A[:], in0=pS[:, 0:128], scalar=w[:, c, j:j + 1], in1=C.tri[:, dd, :],
                             op0=ALU.mult, op1=ALU.mult, R=[pSb, wb_, C.constb], W=[Ab_])
                        vw, vwb = Vw[it % NR]
                        S.op("gpsimd", "tensor_scalar", out=vw[:], in0=v[:, c, :], scalar1=w[:, c, j:j + 1], scalar2=None, op0=ALU.mult,
                             R=[vb, wb_], W=[vwb])
                        pH, pHb = C.ps()
                        S.op("tensor", "matmul", pH[:, 0:129], lhsT=q[:, cs], rhs=Cb[:], start=True, stop=False, R=[qb, Cbb], W=[pHb])
                        S.op("tensor", "matmul", pH[:, 0:129], lhsT=A[:], rhs=v[:, c, :], start=False, stop=True, R=[Ab_, vb], W=[pHb])
                        dn, dnb = dnb_[it % NR]
                        S.op("scalar", "activation", out=dn[:, 0:1], in_=pH[:, 128:129], func=ACT.Abs, scale=eb[:, c, j:j + 1],
                             R=[pHb, ebb], W=[dnb])
                        S.op("vector", "tensor_scalar", out=dn[:, 0:1], in0=dn[:, 0:1], scalar1=1.0, scalar2=None, op0=ALU.max, R=[dnb], W=[dnb])
                        S.op("vector", "reciprocal", out=dn[:, 0:1], in_=dn[:, 0:1], R=[dnb], W=[dnb])
                        S.op("vector", "tensor_tensor", out=dn[:, 1:2], in0=dn[:, 0:1], in1=eb[:, c, j:j + 1], op=ALU.mult, R=[dnb, ebb], W=[dnb])
                        if c >= 2 or need_ctx:
                            S.op("scalar", "activation", out=hh[:, c, :], in_=pH[:, 0:128], func=ACT.Copy, scale=dn[:, 1:2], R=[pHb, dnb], W=[hhb])
                        if step < NTILE - 1:
                            pC, pCb = C.ps()
                            S.op("tensor", "matmul", pC[:, 0:129], lhsT=ktk[:, c, :], rhs=vw[:], start=True, stop=True, R=[ktkb, vwb], W=[pCb])
                            S.op("vector", "tensor_tensor", out=Ct[:], in0=pC[:, 0:129], in1=Ca[:], op=ALU.add, R=[pCb, Cab], W=[Ctb])
                            S.op("vector", "tensor_scalar", out=Ca[:], in0=Ct[:], scalar1=ebl[:, c, j:j + 1], scalar2=None, op0=ALU.mult,
                                 R=[Ctb, eblb], W=[Cab])
                            S.op("scalar", "copy", out=Cb[:], in_=Ca[:], R=[Cab], W=[Cbb])
            for hi in range(2):
                j = 2 * pair + hi
                H = hb[hi]
                o, ob = H["o"]
                (hf, hfb), (hk, hkb) = H["h"]
                for c in range(NTILE):
                    if c < 2 and not need_ctx:
                        continue
                    it += 1
                    cs = slice(c * 128, (c + 1) * 128)
                    hs, hsb = hs_[it % 3]
                    S.op("gpsimd", "tensor_tensor", out=hs[:], in0=hf[:, c, :], in1=hk[:, c, :], op=ALU.add, R=[hfb, hkb], W=[hsb])
                    sq, sqb = sq_[it % 3]
                    S.op("gpsimd", "tensor_tensor", out=sq[:], in0=hs[:], in1=hs[:], op=ALU.mult, R=[hsb], W=[sqb])
                    ss, ssb = ss_[it % 3]
                    S.op("vector", "reduce_sum", out=ss[:, 0:1], in_=sq[:], axis=AX.X, R=[sqb], W=[ssb])
                    S.op("scalar", "activation", out=ss[:, 0:1], in_=ss[:, 0:1], func=ACT.Ln, bias=EPS, scale=1.0 / 128, R=[ssb], W=[ssb])
                    S.op("scalar", "activation", out=ss[:, 0:1], in_=ss[:, 0:1], func=ACT.Exp, scale=-0.5, R=[ssb], W=[ssb])
                    sg, sgb = sg_[it % 3]
                    S.op("scalar", "activation", out=sg[:], in_=o[:, c, :], func=ACT.Sigmoid, R=[ob], W=[sgb])
                    y, yb = y_[it % 3]
                    S.op("vector", "scalar_tensor_tensor", out=y[:], in0=hs[:], scalar=ss[:, 0:1], in1=gain[:, j * 128:(j + 1) * 128],
                         op0=ALU.mult, op1=ALU.mult, R=[hsb, ssb, gainb], W=[yb])
                    S.op("gpsimd", "tensor_tensor", out=y[:], in0=y[:], in1=sg[:], op=ALU.mult, R=[yb, sgb], W=[yb])
                    pT, pTb = C.ps()
                    S.op("tensor", "transpose", pT[:, 0:128], y[:], C.ident[:], R=[yb, C.constb], W=[pTb])
                    sb, sbb = stg.get()
                    S.op("scalar", "copy", out=sb[:], in_=pT[:, 0:128], R=[pTb], W=[sbb])
                    S.dma(C.ldq(), SC["mlsT"][j * 128:(j + 1) * 128, cs], sb[:], reads=[sbb])


def stage_merge(C, I, l, SC, xsrc, xdst, G, Gb, need_ctx):
    S = C.S
    with Stage(C) as st:
        wp = make_wpool(st, kc=KC, ncols=1024, nbuf=1)
        wbr = []
        for r in range(3):
            t, tb = st.tile([128, 4, 1024], BF16, "wbr")
            wsrc = I["w_branch"][l, r].rearrange("(k p) n -> p k n", p=128)
            wf, wfb = wp["f"][0]
            S.dma("sync", wf[:, 0:2, :], wsrc[:, 0:2, :], writes=[wfb])
            S.dma("gpsimd", wf[:, 2:4, :], wsrc[:, 2:4, :], writes=[wfb])
            S.op("gpsimd", "tensor_copy", out=t[:, 0:2, :], in_=wf[:, 0:2, :], R=[wfb], W=[tb])
            S.op("vector", "tensor_copy", out=t[:, 2:4, :], in_=wf[:, 2:4, :], R=[wfb], W=[tb])
            wbr.append((t, tb))
        wo, wob = load_weight_bf16(C, I["w_out"][l].rearrange("(k p) n -> p k n", p=128), 1024, wp)
        brs = [[st.tile([128, 4, 512], BF16, "br") for _ in range(3)] for _ in range(2)]
        gts = [st.tile([128, 24, 512], BF16, "gts") for _ in range(1)]
        xts = [st.tile([128, KC, 512], F32, "xts") for _ in range(1)]
        ym, ymb = st.tile([128, KC, 512], BF16, "ym")
        acc = [[st.tile([128, 512], F32, "acc") for _ in range(3)] for _ in range(2)]
        xo = [st.tile([128, 512], F32, "xo") for _ in range(3)]
        srcs = (SC["attT"], SC["mlsT"], SC["hyT"])
        xv = xsrc.rearrange("(k p) n -> p k n", p=128)
        xdv = xdst.rearrange("(k p) n -> p k n", p=128)
        it = 0
        for bi, (t0, n) in enumerate(BLOCKS):
            if bi == 0 and not need_ctx:
                continue
            w = 1 if bi == 0 else 0
            br = brs[bi % 2]
            for r in range(3):
                S.dma("sync" if r != 1 else "gpsimd", br[r][0][:, :, 0:n], srcs[r].rearrange("(k p) n -> p k n", p=128)[:, :, t0:t0 + n],
                      reads=[C.scrb], writes=[br[r][1]])
            g, gbf = gts[0]
            S.dma("sync", g[:, 0:12, 0:n], SC["bgT"].rearrange("(k p) n -> p k n", p=128)[:, 0:12, t0:t0 + n], reads=[C.scrb], writes=[gbf])
            S.dma("gpsimd", g[:, 12:24, 0:n], SC["bgT"].rearrange("(k p) n -> p k n", p=128)[:, 12:24, t0:t0 + n], reads=[C.scrb], writes=[gbf])
            xt, xtb = xts[0]
            S.dma("sync", xt[:, :, 0:n], xv[:, :, t0:t0 + n], reads=[C.scrb], writes=[xtb])
            for oc in range(KC):
                it += 1
                a = acc[it % 2]
                for r in range(3):
                    pst, psb = C.ps()
                    for k in range(4):
                        S.op("tensor", "matmul", pst[:, 0:n], lhsT=wbr[r][0][:, k, oc * 128:(oc + 1) * 128], rhs=br[r][0][:, k, 0:n],
                             start=(k == 0), stop=(k == 3), R=[wbr[r][1], br[r][1]], W=[psb])
                    S.op("vector", "tensor_tensor", out=a[r][0][:, 0:n], in0=pst[:, 0:n], in1=g[:, r * 8 + oc, 0:n], op=ALU.mult,
                         R=[psb, gbf], W=[a[r][1]])
                S.op("gpsimd", "tensor_tensor", out=a[0][0][:, 0:n], in0=a[0][0][:, 0:n], in1=a[1][0][:, 0:n], op=ALU.add,
                     R=[a[0][1], a[1][1]], W=[a[0][1]])
                S.op("gpsimd", "tensor_tensor", out=ym[:, oc, 0:n], in0=a[0][0][:, 0:n], in1=a[2][0][:, 0:n], op=ALU.add,
                     R=[a[0][1], a[2][1]], W=[ymb])
            for oc in range(KC):
                pst, psb = C.ps()
                for k in range(KC):
                    S.op("tensor", "matmul", pst[:, 0:n], lhsT=wo[:, k, oc * 128:(oc + 1) * 128], rhs=ym[:, k, 0:n],
                         start=(k == 0), stop=(k == KC - 1), R=[wob, ymb], W=[psb])
                it += 1
                o, ob = xo[it % 3]
                S.op("vector", "scalar_tensor_tensor", out=o[:, 0:n], in0=pst[:, 0:n], scalar=G[:, oc, w:w + 1], in1=xt[:, oc, 0:n],
                     op0=ALU.mult, op1=ALU.add, R=[psb, Gb, xtb], W=[ob])
                S.dma(C.ldq(), xdv[:, oc, t0:t0 + n], o[:, 0:n], reads=[ob], writes=[C.scrb])


def stage_ffn_up(C, I, l, SC, xs, A, B, Ab, need_ctx):
    S = C.S
    blocks = [(bi, b) for bi, b in enumerate(BLOCKS) if bi > 0 or need_ctx]
    with Stage(C) as st:
        hT, hTb = st.tile([128, KC, NT], BF16, "hT2")
        norm_mod(C, st, xs, hT, hTb, A, B, Ab, blocks)
        wpool = make_wpool(st, ncols=256, nbuf=1)
        wv = I["w_up"][l].rearrange("(k p) n -> p k n", p=128)
        cw, cwb = st.tile([128, 3, 44], F32, "cw")
        S.dma("sync", cw[:], I["fcwT"][:, l, :, :], writes=[cwb])
        cbias, cbb = st.tile([128, 44], F32, "cbias")
        S.dma("sync", cbias[:], I["fcbT"][:, l, :], writes=[cbb])
        ug, ugb = st.tile([128, L + 2], F32, "ug")
        uv, uvb = st.tile([128, L + 2], F32, "uv")
        cg, cgb = st.tile([128, L], F32, "cg")
        cv, cvb = st.tile([128, L], F32, "cv")
        ab = [st.tile([128, L], BF16, "abf") for _ in range(1)]
        it = 0
        for j in range(22):
            i = wpool["i"]
            wpool["i"] += 1
            wf, wfb = wpool["f"][0]
            wb, wbb = wpool["b"][0]
            S.dma("sync", wf[:, :, 0:128], wv[:, :, j * 128:(j + 1) * 128], writes=[wfb])
            S.dma("gpsimd", wf[:, :, 128:256], wv[:, :, DFF + j * 128:DFF + (j + 1) * 128], writes=[wfb])
            S.op("gpsimd", "tensor_copy", out=wb[:, :, 0:128], in_=wf[:, :, 0:128], R=[wfb], W=[wbb])
            S.op("vector", "tensor_copy", out=wb[:, :, 128:256], in_=wf[:, :, 128:256], R=[wfb], W=[wbb])
            for seq in ((0, LC), (LC, L)):
                s0, ns = seq
                if s0 == 0 and not need_ctx:
                    continue
                it += 1
                S.op("gpsimd", "memset", ug[:, 0:1], 0.0, W=[ugb])
                S.op("gpsimd", "memset", ug[:, ns + 1:ns + 2], 0.0, W=[ugb])
                S.op("gpsimd", "memset", uv[:, 0:1], 0.0, W=[uvb])
                S.op("gpsimd", "memset", uv[:, ns + 1:ns + 2], 0.0, W=[uvb])
                for bi, (t0, n) in blocks:
                    if not (s0 <= t0 < s0 + ns):
                        continue
                    off = t0 - s0
                    for half, (dst, dstb) in enumerate(((ug, ugb), (uv, uvb))):
                        pst, psb = C.ps()
                        for k in range(KC):
                            S.op("tensor", "matmul", pst[:, 0:n], lhsT=wb[:, k, half * 128:(half + 1) * 128], rhs=hT[:, k, t0:t0 + n],
                                 start=(k == 0), stop=(k == KC - 1), R=[wbb, hTb], W=[psb])
                        if half:
                            S.op("vector", "tensor_copy", out=dst[:, 1 + off:1 + off + n], in_=pst[:, 0:n], R=[psb], W=[dstb])
                        else:
                            S.op("scalar", "copy", out=dst[:, 1 + off:1 + off + n], in_=pst[:, 0:n], R=[psb], W=[dstb])
                for half, (src, srcb, dst, dstb) in enumerate(((ug, ugb, cg, cgb), (uv, uvb, cv, cvb))):
                    f = j + 22 * half
                    S.op("scalar", "activation", out=dst[:, 0:ns], in_=src[:, 0:ns], func=ACT.Identity, scale=cw[:, 0, f:f + 1],
                         bias=cbias[:, f:f + 1], R=[srcb, cwb, cbb], W=[dstb])
                    S.op("vector", "scalar_tensor_tensor", out=dst[:, 0:ns], in0=src[:, 1:ns + 1], scalar=cw[:, 1, f:f + 1], in1=dst[:, 0:ns],
                         op0=ALU.mult, op1=ALU.add, R=[srcb, cwb, dstb], W=[dstb])
                    S.op("vector", "scalar_tensor_tensor", out=dst[:, 0:ns], in0=src[:, 2:ns + 2], scalar=cw[:, 2, f:f + 1], in1=dst[:, 0:ns],
                         op0=ALU.mult, op1=ALU.add, R=[srcb, cwb, dstb], W=[dstb])
                S.op("scalar", "activation", out=cg[:, 0:ns], in_=cg[:, 0:ns], func=ACT.Silu, R=[cgb], W=[cgb])
                a, abb = ab[0]
                S.op("gpsimd", "tensor_tensor", out=a[:, 0:ns], in0=cg[:, 0:ns], in1=cv[:, 0:ns], op=ALU.mult, R=[cgb, cvb], W=[abb])
                S.dma(C.ldq(), SC["aT"][j * 128:(j + 1) * 128, s0:s0 + ns], a[:, 0:ns], reads=[abb], writes=[C.scrb])


def stage_ffn_down(C, I, l, SC, xs, G, Gb, need_ctx):
    S = C.S
    with Stage(C) as st:
        wd, wdb = st.tile([128, 22, 1024], BF16, "wd")
        wf = [st.tile([128, 11, 512], F32, "wdf") for _ in range(2)]
        wsrc = I["w_down"][l].rearrange("(k p) n -> p k n", p=128)
        i = 0
        for kh in range(2):
            for ch in range(2):
                f, fb = wf[i % 2]
                i += 1
                S.dma("sync" if i % 2 else "gpsimd", f[:], wsrc[:, kh * 11:(kh + 1) * 11, ch * 512:(ch + 1) * 512], writes=[fb])
                S.op("gpsimd" if i % 2 else "vector", "tensor_copy", out=wd[:, kh * 11:(kh + 1) * 11, ch * 512:(ch + 1) * 512], in_=f[:],
                     R=[fb], W=[wdb])
        ats = [st.tile([128, 22, 512], BF16, "at") for _ in range(2)]
        xts = [st.tile([128, KC, 512], F32, "xts") for _ in range(2)]
        xo = [st.tile([128, 512], F32, "xo") for _ in range(3)]
        xv = xs.rearrange("(k p) n -> p k n", p=128)
        av = SC["aT"].rearrange("(k p) n -> p k n", p=128)
        it = 0
        for bi, (t0, n) in enumerate(BLOCKS):
            if bi == 0 and not need_ctx:
                continue
            w = 1 if bi == 0 else 0
            a, ab_ = ats[bi % 2]
            S.dma("sync", a[:, 0:11, 0:n], av[:, 0:11, t0:t0 + n], reads=[C.scrb], writes=[ab_])
            S.dma("gpsimd", a[:, 11:22, 0:n], av[:, 11:22, t0:t0 + n], reads=[C.scrb], writes=[ab_])
            xt, xtb = xts[bi % 2]
            S.dma("sync", xt[:, :, 0:n], xv[:, :, t0:t0 + n], reads=[C.scrb], writes=[xtb])
            for oc in range(KC):
                pst, psb = C.ps()
                for k in range(22):
                    S.op("tensor", "matmul", pst[:, 0:n], lhsT=wd[:, k, oc * 128:(oc + 1) * 128], rhs=a[:, k, 0:n],
                         start=(k == 0), stop=(k == 21), R=[wdb, ab_], W=[psb])
                it += 1
                o, ob = xo[it % 3]
                S.op("vector", "scalar_tensor_tensor", out=o[:, 0:n], in0=pst[:, 0:n], scalar=G[:, oc, w:w + 1], in1=xt[:, oc, 0:n],
                     op0=ALU.mult, op1=ALU.add, R=[psb, Gb, xtb], W=[ob])
                S.dma(C.ldq(), xv[:, oc, t0:t0 + n], o[:, 0:n], reads=[ob], writes=[C.scrb])


def stage_final(C, I, xs, outT, fg, fgb):
    S = C.S
    with Stage(C) as st:
        xb = [st.tile([128, KC, 512], F32, "xblk") for _ in range(2)]
        sqs = [st.tile([128, 512], BF16, "sq") for _ in range(2)]
        rs, rsb = st.tile([128, 512], F32, "rstd")
        ob_ = [st.tile([128, KC, 512], F32, "oblk") for _ in range(2)]
        xv = xs.rearrange("(k p) n -> p k n", p=128)
        ov = outT.rearrange("(k p) n -> p k n", p=128)
        for bi, (t0, n) in enumerate(BLOCKS):
            if bi == 0:
                continue
            xt, xtb = xb[bi % 2]
            S.dma("sync", xt[:, 0:4, 0:n], xv[:, 0:4, t0:t0 + n], reads=[C.scrb], writes=[xtb])
            S.dma("gpsimd", xt[:, 4:8, 0:n], xv[:, 4:8, t0:t0 + n], reads=[C.scrb], writes=[xtb])
            rms_rstd(C, st, xt, xtb, n, rs, rsb, sqs)
            o, ob = ob_[bi % 2]
            for k in range(KC):
                S.op("vector", "scalar_tensor_tensor", out=o[:, k, 0:n], in0=xt[:, k, 0:n], scalar=fg[:, k:k + 1],
                     in1=rs[:, 0:n], op0=ALU.mult, op1=ALU.mult, R=[xtb, rsb, fgb], W=[ob])
            S.dma("sync", ov[:, 0:4, t0 - LC:t0 - LC + n], o[:, 0:4, 0:n], reads=[ob], writes=[C.outb])
            S.dma("gpsimd", ov[:, 4:8, t0 - LC:t0 - LC + n], o[:, 4:8, 0:n], reads=[ob], writes=[C.outb])
HSEQ = {
    "lat": dict(Ls=L, col0=LC, NTs=L // 128, NH=L // 256, TBm=256, NBt=(L // 2) // 256),
    "ctx": dict(Ls=LC, col0=0, NTs=LC // 128, NH=LC // 256, TBm=128, NBt=1),
}
TWO_PI = 2.0 * math.pi


def hyena_inputs(inp):
    for s, q in HSEQ.items():
        inp("hyF_" + s, [q["NH"], 128, 2, 2, q["NH"], 128], BF16)
        inp("hyG_" + s, [q["NBt"], 128, 2, 2, q["NH"], q["TBm"]], BF16)
        inp("hyz_" + s, [33, q["Ls"]])
        inp("hytn_" + s, [128, q["NTs"]])
    inp("hydel", [128, 512])
    inp("hy_w1", [DEPTH, 33, 64])
    inp("hy_w2", [DEPTH, 64, 64])
    inp("hy_w3", [DEPTH, 64, 2048])
    inp("hy_fb", [64, DEPTH, 4])
    inp("hy_skipT", [128, DEPTH, 2, 4])
    inp("hy_swT", [128, DEPTH, 3, 12])
    inp("hy_sbT", [128, DEPTH, 12])


def hyena_scratch(scr):
    scr("hyc", [1536, NT], F32)
    scr("hfil", [L, 2048], BF16)
    scr("hspec", [4, 2, 2, L // 2, 512], F32)
    scr("hz", [512, NT], F32)


def hyena_host(inputs, m):
    cst = _consts()
    if "hy" not in cst:
        hy = {}
        for s, q in HSEQ.items():
            Ls = q["Ls"]
            n = 2 * Ls
            NH, TBm, NBt, NTs = q["NH"], q["TBm"], q["NBt"], q["NTs"]
            mm = np.arange(Ls // 2, dtype=np.float64)
            f = np.arange(Ls // 2, dtype=np.float64)
            T = np.empty((2, 2, Ls // 2, Ls // 2), np.float64)
            for e in range(2):
                ang = (TWO_PI / n) * np.outer(2 * mm + e, f + 0.5)
                T[0, e] = np.cos(ang)
                T[1, e] = np.sin(ang)
            Fh = T.reshape(2, 2, NH, 128, NH, 128).transpose(4, 3, 0, 1, 2, 5)
            hy["hyF_" + s] = np.ascontiguousarray(Fh).astype(ml_dtypes.bfloat16)
            Tg = T.copy()
            Tg[0] *= (2.0 / n)
            Tg[1] *= -(2.0 / n)
            Gh = Tg.reshape(2, 2, NBt, TBm, NH, 128).transpose(2, 5, 1, 0, 4, 3)
            hy["hyG_" + s] = np.ascontiguousarray(Gh).astype(ml_dtypes.bfloat16)
            t32 = np.arange(Ls, dtype=np.float32)
            tn = (t32 / np.float32(max(Ls - 1, 1))).astype(np.float32)
            w = (np.float32(TWO_PI) * t32 / np.float32(Ls)).astype(np.float32)
            bands = np.linspace(1e-4, 15, 16, dtype=np.float32)
            z = np.concatenate([tn[:, None], np.cos(w[:, None] * bands), -np.sin(w[:, None] * bands)], axis=-1).astype(np.float32)
            hy["hyz_" + s] = np.ascontiguousarray(z.T)
            hy["hytn_" + s] = np.ascontiguousarray((-tn).reshape(NTs, 128).T)
        deltas = np.abs(np.linspace(math.log(1e-2) / 1.5, math.log(1e-2) / 0.3, 512, dtype=np.float32)).astype(np.float32)
        hy["hydel"] = _rep(deltas)
        cst["hy"] = hy
    m.update(cst["hy"])
    for k in ("hy_w1", "hy_w2", "hy_w3"):
        m[k] = np.ascontiguousarray(inputs[k], dtype=np.float32)
    fr = np.asarray(inputs["hy_freq"], np.float32)
    fb = np.stack([fr[:, 0], np.asarray(inputs["hy_b1"], np.float32), fr[:, 1], np.asarray(inputs["hy_b2"], np.float32)], axis=-1)
    m["hy_fb"] = np.ascontiguousarray(fb.transpose(1, 0, 2))
    m["hy_skipT"] = _pk(inputs["hy_skip"])
    m["hy_swT"] = _pk(inputs["hy_short_w"])
    m["hy_sbT"] = _pk(inputs["hy_short_b"])


class HyFwd:
    def __init__(self, C, st, s):
        self.C, self.s = C, s
        NH = HSEQ[s]["NH"]
        self.Ft = [st.tile([128, 4, NH, 128], BF16, "Ft") for _ in range(2)]
        self.ob = [st.tile([128, 512], F32, "fo") for _ in range(4)]
        self.res = [[st.tile([128, 512], F32, "fr") for _ in range(4)] for _ in range(2)]
        self.i = 0

    def run(self, I, X, Xb, consume):
        C, s = self.C, self.s
        S = C.S
        NH = HSEQ[s]["NH"]
        for fc in range(NH):
            self.i += 1
            ft, ftb = self.Ft[self.i % 2]
            src = I["hyF_" + s][fc].rearrange("p a e k f -> p (a e) k f")
            S.dma("sync", ft[:, 0:2], src[:, 0:2], writes=[ftb])
            S.dma("gpsimd", ft[:, 2:4], src[:, 2:4], writes=[ftb])
            pp = []
            for pe in range(4):
                pst, psb = C.ps()
                for k in range(NH):
                    S.op("tensor", "matmul", pst[:, :], lhsT=ft[:, pe, k, :], rhs=X[:, pe % 2, k, :], start=(k == 0), stop=(k == NH - 1),
                         R=[ftb, Xb], W=[psb])
                pp.append((pst, psb))
            (pAe, pAeb), (pAo, pAob), (pBe, pBeb), (pBo, pBob) = pp
            ao, aob = self.ob[(2 * self.i) % 4]
            bo, bob = self.ob[(2 * self.i + 1) % 4]
            S.op("scalar", "copy", out=ao[:], in_=pAo[:, :], R=[pAob], W=[aob])
            S.op("scalar", "copy", out=bo[:], in_=pBo[:, :], R=[pBob], W=[bob])
            r = self.res[self.i % 2]
            S.op("vector", "tensor_tensor", out=r[0][0][:], in0=pAe[:, :], in1=ao[:], op=ALU.add, R=[pAeb, aob], W=[r[0][1]])
            S.op("vector", "tensor_tensor", out=r[2][0][:], in0=pAe[:, :], in1=ao[:], op=ALU.subtract, R=[pAeb, aob], W=[r[2][1]])
            S.op("vector", "tensor_tensor", out=r[1][0][:], in0=pBe[:, :], in1=bo[:], op=ALU.add, R=[pBeb, bob], W=[r[1][1]])
            S.op("vector", "tensor_tensor", out=r[3][0][:], in0=bo[:], in1=pBe[:, :], op=ALU.subtract, R=[pBeb, bob], W=[r[3][1]])
            consume(fc, r[0], r[1], r[2], r[3])


def stage_hyena(C, I, l, SC, need_ctx):
    S = C.S
    with Stage(C) as st:
        sw, swb = st.tile([128, 3, 12], F32, "sw")
        S.dma("sync", sw[:], I["hy_swT"][:, l, :, :], writes=[swb])
        sbi, sbib = st.tile([128, 12], F32, "sbi")
        S.dma("sync", sbi[:], I["hy_sbT"][:, l, :], writes=[sbib])
        ub = [st.tile([128, L + 2], F32, "hu") for _ in range(2)]
        cb = [st.tile([128, L], F32, "hc") for _ in range(2)]
        it = 0
        for ch in range(12):
            for (s0, ns) in ((0, LC), (LC, L)):
                it += 1
                u, ubb = ub[it % 2]
                c, cbb = cb[it % 2]
                S.op("gpsimd", "memset", u[:, 0:1], 0.0, W=[ubb])
                S.op("gpsimd", "memset", u[:, ns + 1:ns + 2], 0.0, W=[ubb])
                S.dma("sync" if it % 2 else "gpsimd", u[:, 1:ns + 1], SC["hyu"][ch * 128:(ch + 1) * 128, s0:s0 + ns], writes=[ubb])
                S.op("scalar", "activation", out=c[:, 0:ns], in_=u[:, 0:ns], func=ACT.Identity, scale=sw[:, 0, ch:ch + 1],
                     bias=sbi[:, ch:ch + 1], R=[ubb, swb, sbib], W=[cbb])
                S.op("vector", "scalar_tensor_tensor", out=c[:, 0:ns], in0=u[:, 1:ns + 1], scalar=sw[:, 1, ch:ch + 1], in1=c[:, 0:ns],
                     op0=ALU.mult, op1=ALU.add, R=[ubb, swb, cbb], W=[cbb])
                S.op("vector", "scalar_tensor_tensor", out=c[:, 0:ns], in0=u[:, 2:ns + 2], scalar=sw[:, 2, ch:ch + 1], in1=c[:, 0:ns],
                     op0=ALU.mult, op1=ALU.add, R=[ubb, swb, cbb], W=[cbb])
                S.dma("sync" if it % 2 else "gpsimd", SC["hyc"][ch * 128:(ch + 1) * 128, s0:s0 + ns], c[:, 0:ns], reads=[cbb])

    for s in (("lat", "ctx") if need_ctx else ("lat",)):
        q = HSEQ[s]
        Ls, col0, NTs, NBt = q["Ls"], q["col0"], q["NTs"], q["NBt"]
        with Stage(C) as so:
            rn, rnb = so.tile([128, 2048], F32, "rn")
            skp, skpb = so.tile([128, 2, 4], F32, "skp")
            S.dma("sync", skp[:], I["hy_skipT"][:, l, :, :], writes=[skpb])
            with Stage(C) as st:
                zT, zTb = st.tile([33, Ls], F32, "zT")
                S.dma("sync", zT[:], I["hyz_" + s][:, :], writes=[zTb])
                w1, w1b = st.tile([33, 64], F32, "w1")
                S.dma("sync", w1[:], I["hy_w1"][l], writes=[w1b])
                w2, w2b = st.tile([64, 64], F32, "w2")
                S.dma("sync", w2[:], I["hy_w2"][l], writes=[w2b])
                w3, w3b = st.tile([64, 2048], F32, "w3")
                S.dma("gpsimd", w3[:], I["hy_w3"][l], writes=[w3b])
                fb, fbb = st.tile([64, 6], F32, "fb")
                S.dma("sync", fb[:, 0:4], I["hy_fb"][:, l, :], writes=[fbb])
                S.op("vector", "tensor_tensor", out=fb[:, 4:5], in0=fb[:, 0:1], in1=fb[:, 1:2], op=ALU.mult, R=[fbb], W=[fbb])
                S.op("vector", "tensor_tensor", out=fb[:, 5:6], in0=fb[:, 2:3], in1=fb[:, 3:4], op=ALU.mult, R=[fbb], W=[fbb])
                tn, tnb = st.tile([128, NTs], F32, "tn")
                S.dma("sync", tn[:], I["hytn_" + s][:, :], writes=[tnb])
                dl, dlb = st.tile([128, 512], F32, "del")
                S.dma("sync", dl[:], I["hydel"][:, :], writes=[dlb])
                h1, h1b = st.tile([64, 512], F32, "h1")
                h2, h2b = st.tile([64, Ls], F32, "h2")
                ta = [st.tile([64, 512], F32, "ta") for _ in range(2)]
                tm_ = [st.tile([64, 512], F32, "tm") for _ in range(2)]
                nb_ = min(512, Ls)
                for bi in range(Ls // nb_):
                    bs = slice(bi * nb_, (bi + 1) * nb_)
                    for (wt, wtb, src, srcb, fcol, bcol, dst, dstb, dsl) in (
                            (w1, w1b, zT[:, bs], zTb, 0, 4, h1, h1b, slice(0, nb_)),
                            (w2, w2b, h1[:, 0:nb_], h1b, 2, 5, h2, h2b, bs)):
                        pst, psb = C.ps()
                        S.op("tensor", "matmul", pst[0:64, 0:nb_], lhsT=wt[:], rhs=src, start=True, stop=True, R=[wtb, srcb], W=[psb])
                        a, ab_ = ta[bi % 2]
                        S.op("vector", "tensor_scalar", out=a[:, 0:nb_], in0=pst[0:64, 0:nb_], scalar1=fb[:, fcol:fcol + 1],
                             scalar2=fb[:, bcol:bcol + 1], op0=ALU.mult, op1=ALU.add, R=[psb, fbb], W=[ab_])
                        m_, mb_ = tm_[bi % 2]
                        for _r in range(2):
                            S.op("vector", "tensor_scalar", out=m_[:, 0:nb_], in0=a[:, 0:nb_], scalar1=math.pi, scalar2=-TWO_PI,
                                 op0=ALU.is_gt, op1=ALU.mult, R=[ab_], W=[mb_])
                            S.op("vector", "tensor_tensor", out=a[:, 0:nb_], in0=a[:, 0:nb_], in1=m_[:, 0:nb_], op=ALU.add, R=[ab_, mb_], W=[ab_])
                            S.op("vector", "tensor_scalar", out=m_[:, 0:nb_], in0=a[:, 0:nb_], scalar1=-math.pi, scalar2=TWO_PI,
                                 op0=ALU.is_lt, op1=ALU.mult, R=[ab_], W=[mb_])
                            S.op("vector", "tensor_tensor", out=a[:, 0:nb_], in0=a[:, 0:nb_], in1=m_[:, 0:nb_], op=ALU.add, R=[ab_, mb_], W=[ab_])
                        S.op("vector", "tensor_scalar", out=a[:, 0:nb_], in0=a[:, 0:nb_], scalar1=-3.1415925, scalar2=3.1415925,
                             op0=ALU.max, op1=ALU.min, R=[ab_], W=[ab_])
                        S.op("scalar", "activation", out=dst[:, dsl], in_=a[:, 0:nb_], func=ACT.Sin, R=[ab_], W=[dstb])
                dec = [st.tile([128, 512], F32, "dec") for _ in range(2)]
                hd = [st.tile([128, 512], BF16, "hd") for _ in range(3)]
                ha = [st.tile([128, 512], BF16, "ha") for _ in range(3)]
                nbank = [4, 5, 6, 7]
                C.reserved = set(nbank)
                it = 0
                for k in range(NTs):
                    d, db = dec[k % 2]
                    S.op("scalar", "activation", out=d[:], in_=dl[:], func=ACT.Exp, scale=tn[:, k:k + 1], R=[dlb, tnb], W=[db])
                    for g in range(4):
                        it += 1
                        pst, psb = C.ps()
                        S.op("tensor", "matmul", pst[:, :], lhsT=h2[:, k * 128:(k + 1) * 128], rhs=w3[:, g * 512:(g + 1) * 512], start=True, stop=True,
                             R=[h2b, w3b], W=[psb])
                        hh, hhb = hd[it % 3]
                        S.op("vector", "tensor_tensor", out=hh[:], in0=pst[:, :], in1=d[:], op=ALU.mult, R=[psb, db], W=[hhb])
                        aa, aab = ha[it % 3]
                        S.op("scalar", "activation", out=aa[:], in_=hh[:], func=ACT.Abs, R=[hhb], W=[aab])
                        S.op("tensor", "matmul", C.psum[nbank[g]][:, :], lhsT=C.ones_bf[:], rhs=aa[:], start=(k == 0), stop=(k == NTs - 1),
                             R=[aab, C.constb], W=[C.psb[nbank[g]]])
                        S.dma(C.ldq(), SC["hfil"][k * 128:(k + 1) * 128, g * 512:(g + 1) * 512], hh[:], reads=[hhb])
                for g in range(4):
                    S.op("vector", "reciprocal", out=rn[:, g * 512:(g + 1) * 512], in_=C.psum[nbank[g]][:, :], R=[C.psb[nbank[g]]], W=[rnb])
                C.reserved = set()
            NH, TBm = q["NH"], q["TBm"]
            with Stage(C) as st:
                Xb_ = [st.tile([128, 2, NH, 512], BF16, "Xf") for _ in range(2)]
                fw = HyFwd(C, st, s)
                for g in range(4):
                    X, Xb = Xb_[g % 2]
                    src = SC["hfil"][0:Ls, g * 512:(g + 1) * 512].rearrange("(k p e) c -> p e k c", p=128, e=2)
                    S.dma("sync", X[:, 0], src[:, 0], writes=[Xb])
                    S.dma("gpsimd", X[:, 1], src[:, 1], writes=[Xb])

                    def consume(fc, Alo, Blo, Ahi, Bhi, g=g):
                        fs = slice(fc * 128, (fc + 1) * 128)
                        for i_, (t_, half, part) in enumerate(((Alo, 0, 0), (Blo, 0, 1), (Ahi, 1, 0), (Bhi, 1, 1))):
                            S.dma("sync" if i_ % 2 else "gpsimd", SC["hspec"][g, part, half, fs, :], t_[0][:], reads=[t_[1]])
                    fw.run(I, X, Xb, consume)
            for o in range(2):
                with Stage(C) as sz:
                    PQ = [sz.tile([128, NH, 512], BF16, "PQ") for _ in range(4)]
                    usrc = SC["hyc"][1024:1536, :] if o == 0 else SC["hz"]
                    gsrc = SC["hyc"][o * 512:(o + 1) * 512, :]
                    with Stage(C) as st:
                        X, Xb = st.tile([128, 2, NH, 512], BF16, "Xd")
                        UW = min(1024, Ls)
                        uf = [st.tile([128, UW], F32, "uf") for _ in range(2)]
                        ui = 0
                        for cc in range(4):
                            for u0 in range(0, Ls, UW):
                                ui += 1
                                u, ub_ = uf[ui % 2]
                                S.dma("sync" if ui % 2 else "gpsimd", u[:], usrc[cc * 128:(cc + 1) * 128, col0 + u0:col0 + u0 + UW], writes=[ub_])
                                nk = UW // 256
                                for e in range(2):
                                    for k0 in range(0, nk, 4):
                                        kn = min(4, nk - k0)
                                        kb0 = u0 // 256 + k0
                                        pt, ptb = C.ps()
                                        for kk in range(kn):
                                            c0_ = (k0 + kk) * 256 + e
                                            S.op("tensor", "transpose", pt[:, kk * 128:(kk + 1) * 128], u[:, c0_:c0_ + 255:2], C.ident[:],
                                                 R=[ub_, C.constb], W=[ptb])
                                        if e:
                                            S.op("vector", "tensor_copy", out=X[:, e, kb0:kb0 + kn, cc * 128:(cc + 1) * 128],
                                                 in_=pt[:, 0:kn * 128].rearrange("p (k c) -> p k c", c=128), R=[ptb], W=[Xb])
                                        else:
                                            S.op("scalar", "copy", out=X[:, e, kb0:kb0 + kn, cc * 128:(cc + 1) * 128],
                                                 in_=pt[:, 0:kn * 128].rearrange("p (k c) -> p k c", c=128), R=[ptb], W=[Xb])
                        sp = [st.tile([128, 512], F32, "sp") for _ in range(4)]
                        KK = [st.tile([128, 512], F32, "KK") for _ in range(2)]
                        tt = [st.tile([128, 512], F32, "tt") for _ in range(4)]
                        Zt = [st.tile([128, 512], F32, "Zt") for _ in range(4)]
                        rf = rn[:, o * 1024:o * 1024 + 512]
                        rb = rn[:, o * 1024 + 512:o * 1024 + 1024]

                        def consume(fc, Alo, Blo, Ahi, Bhi, o=o):
                            fs = slice(fc * 128, (fc + 1) * 128)
                            for half, (A_, B_) in enumerate(((Alo, Blo), (Ahi, Bhi))):
                                for i_, (g_, part) in enumerate(((2 * o, 0), (2 * o, 1), (2 * o + 1, 0), (2 * o + 1, 1))):
                                    S.dma("sync" if i_ % 2 else "gpsimd", sp[i_][0][:], SC["hspec"][g_, part, half, fs, :], writes=[sp[i_][1]])
                                (Af, Afb), (Bf, Bfb), (Ab, Abb), (Bb, Bbb) = sp
                                (Kr, Krb), (Ki, Kib) = KK
                                S.op("gpsimd", "tensor_tensor", out=Af[:], in0=Af[:], in1=rf, op=ALU.mult, R=[Afb, rnb], W=[Afb])
                                S.op("gpsimd", "tensor_tensor", out=Ab[:], in0=Ab[:], in1=rb, op=ALU.mult, R=[Abb, rnb], W=[Abb])
                                S.op("gpsimd", "tensor_tensor", out=Kr[:], in0=Af[:], in1=Ab[:], op=ALU.add, R=[Afb, Abb], W=[Krb])
                                S.op("gpsimd", "tensor_tensor", out=Bf[:], in0=Bf[:], in1=rf, op=ALU.mult, R=[Bfb, rnb], W=[Bfb])
                                S.op("gpsimd", "tensor_tensor", out=Bb[:], in0=Bb[:], in1=rb, op=ALU.mult, R=[Bbb, rnb], W=[Bbb])
                                S.op("gpsimd", "tensor_tensor", out=Ki[:], in0=Bb[:], in1=Bf[:], op=ALU.subtract, R=[Bbb, Bfb], W=[Kib])
                                (t1, t1b), (t2, t2b), (t3, t3b), (t4, t4b) = tt
                                zr, zrb = Zt[2 * half]
                                zi, zib = Zt[2 * half + 1]
                                S.op("vector", "tensor_tensor", out=t1[:], in0=A_[0][:], in1=Kr[:], op=ALU.mult, R=[A_[1], Krb], W=[t1b])
                                S.op("vector", "tensor_tensor", out=t2[:], in0=B_[0][:], in1=Ki[:], op=ALU.mult, R=[B_[1], Kib], W=[t2b])
                                S.op("gpsimd", "tensor_tensor", out=t3[:], in0=A_[0][:], in1=Ki[:], op=ALU.mult, R=[A_[1], Kib], W=[t3b])
                                S.op("gpsimd", "tensor_tensor", out=t4[:], in0=B_[0][:], in1=Kr[:], op=ALU.mult, R=[B_[1], Krb], W=[t4b])
                                S.op("vector", "tensor_tensor", out=zr[:], in0=t1[:], in1=t2[:], op=ALU.add, R=[t1b, t2b], W=[zrb])
                                S.op("vector", "tensor_tensor", out=zi[:], in0=t3[:], in1=t4[:], op=ALU.subtract, R=[t3b, t4b], W=[zib])
                            (zrl, zrlb), (zil, zilb), (zrh, zrhb), (zih, zihb) = Zt
                            S.op("vector", "tensor_tensor", out=PQ[0][0][:, fc, :], in0=zrl[:], in1=zrh[:], op=ALU.add, R=[zrlb, zrhb], W=[PQ[0][1]])
                            S.op("vector", "tensor_tensor", out=PQ[1][0][:, fc, :], in0=zil[:], in1=zih[:], op=ALU.subtract, R=[zilb, zihb], W=[PQ[1][1]])
                            S.op("gpsimd", "tensor_tensor", out=PQ[2][0][:, fc, :], in0=zrl[:], in1=zrh[:], op=ALU.subtract, R=[zrlb, zrhb], W=[PQ[2][1]])
                            S.op("gpsimd", "tensor_tensor", out=PQ[3][0][:, fc, :], in0=zil[:], in1=zih[:], op=ALU.add, R=[zilb, zihb], W=[PQ[3][1]])
                        HyFwd(C, st, s).run(I, X, Xb, consume)
                    with Stage(C) as st:
                        TT = 2 * TBm
                        Gt = [st.tile([128, 2, 2, NH, TBm], BF16, "Gt") for _ in range(2)]
                        ut = [st.tile([128, TT], F32, "ut") for _ in range(3)]
                        gt_ = [st.tile([128, TT], F32, "gt") for _ in range(3)]
                        o1 = [st.tile([128, TT], F32, "o1") for _ in range(3)]
                        ob16 = [st.tile([128, TT], BF16, "ob16") for _ in range(3)]
                        it = 0
                        for tb in range(NBt):
                            G, Gb = Gt[tb % 2]
                            S.dma("sync", G[:, 0], I["hyG_" + s][tb, :, 0], writes=[Gb])
                            S.dma("gpsimd", G[:, 1], I["hyG_" + s][tb, :, 1], writes=[Gb])
                            tsl = slice(col0 + tb * TT, col0 + (tb + 1) * TT)
                            for cc in range(4):
                                it += 1
                                u, ub_ = ut[it % 3]
                                gg, ggb = gt_[it % 3]
                                S.dma("sync", u[:], usrc[cc * 128:(cc + 1) * 128, tsl], writes=[ub_])
                                S.dma("gpsimd", gg[:], gsrc[cc * 128:(cc + 1) * 128, tsl], writes=[ggb])
                                t1, t1b = o1[it % 3]
                                for e in range(2):
                                    pst, psb = C.ps()
                                    Pt, Ptb = PQ[2 * e]
                                    Qt, Qtb = PQ[2 * e + 1]
                                    for fk in range(NH):
                                        S.op("tensor", "matmul", pst[:, 0:TBm], lhsT=Pt[:, fk, cc * 128:(cc + 1) * 128], rhs=G[:, e, 0, fk, :],
                                             start=(fk == 0), stop=False, R=[Ptb, Gb], W=[psb])
                                        S.op("tensor", "matmul", pst[:, 0:TBm], lhsT=Qt[:, fk, cc * 128:(cc + 1) * 128], rhs=G[:, e, 1, fk, :],
                                             start=False, stop=(fk == NH - 1), R=[Qtb, Gb], W=[psb])
                                    S.op("vector", "scalar_tensor_tensor", out=t1[:, e:TT:2], in0=u[:, e:TT:2], scalar=skp[:, o, cc:cc + 1],
                                         in1=pst[:, 0:TBm], op0=ALU.mult, op1=ALU.add, R=[ub_, skpb, psb], W=[t1b])
                                if o == 0:
                                    S.op("gpsimd", "tensor_tensor", out=t1[:], in0=t1[:], in1=gg[:], op=ALU.mult, R=[t1b, ggb], W=[t1b])
                                    S.dma(C.ldq(), SC["hz"][cc * 128:(cc + 1) * 128, tsl], t1[:], reads=[t1b])
                                else:
                                    ob, obb = ob16[it % 3]
                                    S.op("gpsimd", "tensor_tensor", out=ob[:], in0=t1[:], in1=gg[:], op=ALU.mult, R=[t1b, ggb], W=[obb])
                                    S.dma(C.ldq(), SC["hyT"][cc * 128:(cc + 1) * 128, tsl], ob[:], reads=[obb])


ALL_STAGES = ("inproj", "attn", "mlstm", "hyena", "merge", "ffn")


def build_program(n_layers=DEPTH, dbg=None, stages=ALL_STAGES, final=True):
    nc = bass.Bass("TRN2", target_bir_lowering=False)
    I = {}

    def inp(name, shape, dtype=F32):
        I[name] = nc.dram_tensor(name, list(shape), dtype, kind="ExternalInput").ap()

    inp("xT", [D, NT])
    inp("cvec", [128, KC, 2])
    inp("ada_w", [DEPTH, D, 6 * D])
    inp("ada_bT", [128, DEPTH, 48])
    inp("n1gT", [128, DEPTH, KC])
    inp("n2gT", [128, DEPTH, KC])
    inp("fgT", [128, KC])
    inp("w_in", [DEPTH, D, IN_W])
    inp("ropecos", [128, L])
    inp("ropesin", [128, L])
    inp("ident", [128, 128])
    inp("amask", [128, 2, 512], BF16)
    inp("tri", [128, 2, 128])
    inp("sinkR", [128, DEPTH * 8])
    inp("mlgbR", [128, DEPTH * 16])
    inp("mlngR", [128, DEPTH * 512])
    inp("w_branch", [DEPTH, 3, 512, D])
    inp("w_out", [DEPTH, D, D])
    inp("w_up", [DEPTH, D, 2 * DFF])
    inp("w_down", [DEPTH, DFF, D])
    inp("fcwT", [128, DEPTH, 3, 44])
    inp("fcbT", [128, DEPTH, 44])
    hyena_inputs(inp)

    SC = {}

    def scr(name, shape, dtype):
        kind = "ExternalOutput" if (dbg and name in dbg) else "Internal"
        SC[name] = nc.dram_tensor(name, list(shape), dtype, kind=kind).ap()

    scr("qT", [512, NT], BF16)
    scr("kT", [128, NT], BF16)
    scr("vtok", [NT, 128], BF16)
    scr("mlqT", [512, NT], BF16)
    scr("mlkT", [512, NT], BF16)
    scr("mlktok", [NT, 512], BF16)
    scr("mlvtok", [NT, 512], BF16)
    scr("mlotok", [NT, 512], F32)
    scr("mlgtok", [NT, 16], F32)
    scr("hyu", [1536, NT], F32)
    scr("bgT", [3072, NT], BF16)
    scr("attT", [512, NT], BF16)
    scr("mlsT", [512, NT], BF16)
    scr("hyT", [512, NT], BF16)
    scr("aT", [DFF, NT], BF16)
    scr("xs", [D, NT], F32)
    hyena_scratch(scr)
    outT = nc.dram_tensor("outT", [D, L], F32, kind="ExternalOutput").ap()

    with ExitStack() as es:
        S = Sched(nc, es)
        C = Ctx(nc, S)
        C.scrb = None
        C.outb = Buf("out")
        C.psum = []
        C.psb = []
        for i in range(8):
            C.psum.append(es.enter_context(nc.psum_tensor("ps%d" % i, [128, 512], F32)))
            C.psb.append(Buf("ps%d" % i))
        C.constb = Buf("const")
        C.ones_bf = es.enter_context(nc.sbuf_tensor("ones_bf", [128, 128], BF16))
        S.op("gpsimd", "memset", C.ones_bf[:], 1.0, W=[C.constb])
        C.negpi = es.enter_context(nc.sbuf_tensor("negpi", [128, 1], F32))
        S.op("gpsimd", "memset", C.negpi[:], -math.pi, W=[C.constb])
        C.ident = es.enter_context(nc.sbuf_tensor("ident_sb", [128, 128], F32))
        S.dma("sync", C.ident[:], I["ident"][:, :], writes=[C.constb])
        C.amask = es.enter_context(nc.sbuf_tensor("amask_sb", [128, 2, 512], BF16))
        S.dma("sync", C.amask[:], I["amask"][:, :, :], writes=[C.constb])
        C.tri = es.enter_context(nc.sbuf_tensor("tri_sb", [128, 2, 128], F32))
        S.dma("sync", C.tri[:], I["tri"][:, :, :], writes=[C.constb])
        mod = [es.enter_context(nc.sbuf_tensor("mod%d" % l, [128, 48, 2], F32)) for l in range(DEPTH)]
        mod_b = [Buf("mod%d" % l) for l in range(DEPTH)]
        n1g = es.enter_context(nc.sbuf_tensor("n1g", [128, DEPTH, KC], F32))
        n2g = es.enter_context(nc.sbuf_tensor("n2g", [128, DEPTH, KC], F32))
        fg = es.enter_context(nc.sbuf_tensor("fg", [128, KC], F32))
        ngb = Buf("ng")
        S.dma("sync", n1g[:], I["n1gT"][:, :, :], writes=[ngb])
        S.dma("sync", n2g[:], I["n2gT"][:, :, :], writes=[ngb])
        S.dma("sync", fg[:], I["fgT"][:, :], writes=[ngb])
        AB = es.enter_context(nc.sbuf_tensor("AB", [128, DEPTH, 2, KC, 2], F32))
        ABb = Buf("AB")

        stage_mod(C, I, mod, mod_b)
        for l in range(DEPTH):
            for which, ng, off in ((0, n1g, 8), (1, n2g, 32)):
                for w in range(2):
                    S.op("vector", "scalar_tensor_tensor", out=AB[:, l, which, :, w], in0=mod[l][:, off:off + 8, w], scalar=1.0,
                         in1=ng[:, l, :], op0=ALU.add, op1=ALU.mult, R=[mod_b[l], ngb], W=[ABb])
        S.barrier()

        for l in range(n_layers):
            need_ctx = l < DEPTH - 1
            xsrc = I["xT"] if l == 0 else SC["xs"]
            if "inproj" in stages:
                stage_inproj(C, I, l, xsrc, AB[:, l, 0, :, :], mod[l][:, 0:8, :], ABb, SC)
            if "attn" in stages:
                stage_attn(C, I, l, SC, need_ctx)
            if "mlstm" in stages:
                stage_mlstm(C, I, l, SC, need_ctx)
            if "hyena" in stages:
                stage_hyena(C, I, l, SC, need_ctx)
            if "merge" in stages:
                stage_merge(C, I, l, SC, xsrc, SC["xs"], mod[l][:, 16:24, :], mod_b[l], need_ctx)
            if "ffn" in stages:
                stage_ffn_up(C, I, l, SC, SC["xs"], AB[:, l, 1, :, :], mod[l][:, 24:32, :], ABb, need_ctx)
                stage_ffn_down(C, I, l, SC, SC["xs"], mod[l][:, 40:48, :], mod_b[l], need_ctx)
        if final:
            stage_final(C, I, SC["xs"], outT, fg, ngb)
        else:
            with Stage(C) as st:
                t, tb = st.tile([128, 64], F32, "o")
                S.op("vector", "tensor_copy", out=t[:], in_=mod[0][:, 0:32, :].rearrange("p a b -> p (a b)"), R=[mod_b[0]], W=[tb])
                S.dma("sync", outT[0:128, 0:64], t[:], reads=[tb], writes=[C.outb])
        S.barrier()
        S.emit()
    return nc, SC


def _rope_tables():
    rows = L // 64
    row = np.repeat(np.arange(rows, dtype=np.float32), 64)
    col = np.tile(np.arange(64, dtype=np.float32), rows)
    nf = 16
    inv = (np.float32(10000.0) ** (-np.arange(nf, dtype=np.float32) / nf)).astype(np.float32)
    ang = np.concatenate([row[:, None] * inv, col[:, None] * inv], axis=-1).astype(np.float32)
    c = np.cos(ang).T.astype(np.float32)
    s = np.sin(ang).T.astype(np.float32)
    c64 = np.concatenate([c, c], 0)
    s64 = np.concatenate([-s, s], 0)
    return np.ascontiguousarray(np.concatenate([c64, c64], 0)), np.ascontiguousarray(np.concatenate([s64, s64], 0))


def _pk(v):
    v = np.asarray(v, np.float32)
    lead = v.shape[:-1]
    k = v.shape[-1] // 128
    return np.ascontiguousarray(np.moveaxis(v.reshape(*lead, k, 128), -1, 0))


def _rep(v):
    v = np.asarray(v, np.float32).reshape(1, -1)
    return np.ascontiguousarray(np.repeat(v, 128, axis=0))


_CONST = {}


def _consts():
    if _CONST:
        return _CONST
    _CONST["rope"] = _rope_tables()
    i = np.arange(128)
    mp = (i[:, None] >= i[None, :]).astype(np.float32)
    mn = (i[:, None] <= i[None, :]).astype(np.float32)
    am = np.stack([np.tile(mp, (1, 4)), np.tile(mn, (1, 4))], axis=1)
    _CONST["amask"] = np.ascontiguousarray(am).astype(ml_dtypes.bfloat16)
    _CONST["tri"] = np.ascontiguousarray(np.stack([mn, mp], axis=1)).astype(np.float32)
    _CONST["ident"] = np.eye(128, dtype=np.float32)
    return _CONST


def prep_shared_inputs(inputs):
    cst = _consts()
    m = {}
    for k in ("ada_w", "w_in", "w_branch", "w_out", "w_up", "w_down"):
        m[k] = np.ascontiguousarray(inputs[k], dtype=np.float32)
    m["ada_bT"] = _pk(inputs["ada_b"])
    m["n1gT"] = _pk(inputs["norm1_g"])
    m["n2gT"] = _pk(inputs["norm2_g"])
    m["fgT"] = _pk(inputs["final_g"])
    m["ropecos"], m["ropesin"] = cst["rope"]
    m["ident"] = cst["ident"]
    m["amask"] = cst["amask"]
    m["tri"] = cst["tri"]
    m["sinkR"] = _rep(inputs["att_sink"])
    m["mlgbR"] = _rep(inputs["ml_gate_b"])
    m["mlngR"] = _rep(inputs["ml_norm_g"])
    m["fcwT"] = _pk(inputs["ffn_conv_w"])
    m["fcbT"] = _pk(inputs["ffn_conv_b"])
    hyena_host(inputs, m)
    return m


def prep_core_inputs(inputs, b, shared=None):
    m = dict(shared if shared is not None else prep_shared_inputs(inputs))
    xcat = np.concatenate([inputs["ctx"][b], inputs["x"][b]], axis=0)
    m["xT"] = np.ascontiguousarray(xcat.T, dtype=np.float32)
    cv = np.stack([inputs["c"][b], inputs["c_ctx"]], axis=-1)
    m["cvec"] = np.ascontiguousarray(cv.reshape(KC, 128, 2).transpose(1, 0, 2), dtype=np.float32)
    return m


_PROG = {}


def kernel(**inputs):
    inputs = {k: np.asarray(v) for k, v in inputs.items()}
    if "nc" not in _PROG:
        _PROG["nc"] = build_program()[0]
    nc = _PROG["nc"]
    shared = prep_shared_inputs(inputs)
    B = inputs["x"].shape[0]
    in_maps = [prep_core_inputs(inputs, c % B, shared) for c in range(8)]
    res = run_bass_kernel_spmd(nc, in_maps, core_ids=list(range(8)))
    out = np.stack([np.asarray(res.results[b]["outT"], dtype=np.float32).T for b in range(B)], axis=0)
    return np.ascontiguousarray(out)
```
